# Optimizing a Trainium2 kernel written in Bass

```python
import jax, jax.numpy as jnp
from jax import lax
import numpy as np

D_MODEL = 1024
BATCH = 2
SEQ = 8192
DEPTH = 2

GRID_W = 64
CTX_LEN = 256
N_EVEN = (DEPTH + 1) // 2
N_ODD = DEPTH // 2

MLA_HEADS = 8
QK_NOPE_DIM = 64
QK_ROPE_DIM = 32
QK_HEAD_DIM = QK_NOPE_DIM + QK_ROPE_DIM
V_HEAD_DIM = 64
Q_LORA_RANK = 384
KV_LORA_RANK = 256
QK_SCALE = QK_HEAD_DIM ** -0.5
ROPE_BASE = 10000.0
BLOCK_Q = 128
CONV_GROUPS = 8
CONV_GROUP_DIM = 64
CONV_DIM = CONV_GROUPS * CONV_GROUP_DIM
CONV_WIDTH = 3
CONV_OFFSET = Q_LORA_RANK + KV_LORA_RANK + QK_ROPE_DIM
IN_PROJ_DIM = CONV_OFFSET + 3 * CONV_DIM
MIX_OUT_DIM = MLA_HEADS * V_HEAD_DIM + CONV_DIM
FOURIER_GROUPS = 4
FOURIER_GROUP_DIM = D_MODEL // FOURIER_GROUPS
D_FF = ((8 * D_MODEL + 3 * 256 - 1) // (3 * 256)) * 256
EPS = 1e-6

kernel_name = "hybrid_mla_shortconv_fnet_dit_block"


def rmsnorm(x, g):
    xf = x.astype(jnp.float32)
    y = xf * lax.rsqrt(jnp.mean(xf * xf, axis=-1, keepdims=True) + EPS)
    return (y * g.astype(jnp.float32)).astype(x.dtype)


def axial_angles(T):
    rows = T // GRID_W
    row = jnp.broadcast_to(jnp.arange(rows)[:, None], (rows, GRID_W)).reshape(-1).astype(jnp.float32)
    col = jnp.broadcast_to(jnp.arange(GRID_W)[None, :], (rows, GRID_W)).reshape(-1).astype(jnp.float32)
    half = QK_ROPE_DIM // 2
    inv = 1.0 / (ROPE_BASE ** (jnp.arange(0, half, 2, dtype=jnp.float32) / half))
    return row[:, None] * inv, col[:, None] * inv


def rotate(x, ang):
    cos = jnp.cos(ang)[:, None, :].astype(x.dtype)
    sin = jnp.sin(ang)[:, None, :].astype(x.dtype)
    x1, x2 = jnp.split(x, 2, axis=-1)
    return jnp.concatenate([x1 * cos - x2 * sin, x2 * cos + x1 * sin], axis=-1)


def axial_rope(x, ang_row, ang_col):
    half = QK_ROPE_DIM // 2
    return jnp.concatenate([rotate(x[..., :half], ang_row), rotate(x[..., half:], ang_col)], axis=-1)


def mla_qkv(proj, q_norm_g, kv_norm_g, w_uq, w_ukv, q_gain, k_gain, angles):
    B, T, _ = proj.shape
    cq = proj[..., :Q_LORA_RANK]
    ckv = proj[..., Q_LORA_RANK:Q_LORA_RANK + KV_LORA_RANK]
    k_pe = proj[..., Q_LORA_RANK + KV_LORA_RANK:CONV_OFFSET]
    q = (rmsnorm(cq, q_norm_g) @ w_uq).reshape(B, T, MLA_HEADS, QK_HEAD_DIM)
    kv = (rmsnorm(ckv, kv_norm_g) @ w_ukv).reshape(B, T, MLA_HEADS, QK_NOPE_DIM + V_HEAD_DIM)
    k_nope, v = kv[..., :QK_NOPE_DIM], kv[..., QK_NOPE_DIM:]
    k = jnp.concatenate([k_nope, jnp.broadcast_to(k_pe[:, :, None, :], (B, T, MLA_HEADS, QK_ROPE_DIM))], axis=-1)
    q = rmsnorm(q, q_gain)
    k = rmsnorm(k, k_gain)
    if angles is not None:
        ang_row, ang_col = angles
        q = jnp.concatenate([q[..., :QK_NOPE_DIM], axial_rope(q[..., QK_NOPE_DIM:], ang_row, ang_col)], axis=-1)
        k = jnp.concatenate([k[..., :QK_NOPE_DIM], axial_rope(k[..., QK_NOPE_DIM:], ang_row, ang_col)], axis=-1)
    return q, k, v


def attend(q, k, v):
    s = jnp.einsum('bqhd,bkhd->bhqk', q, k, preferred_element_type=jnp.float32) * QK_SCALE
    p = jax.nn.softmax(s, axis=-1)
    return jnp.einsum('bhqk,bkhd->bqhd', p.astype(v.dtype), v)


def attend_blocked(q, k, v):
    B, S, H, D = q.shape
    nb = S // BLOCK_Q
    qb = q.reshape(B, nb, BLOCK_Q, H, D).transpose(1, 0, 2, 3, 4)
    ob = lax.map(lambda qi: attend(qi, k, v), qb)
    return ob.transpose(1, 0, 2, 3, 4).reshape(B, S, H * V_HEAD_DIM)


def short_conv(p, conv_w):
    b_gate, c_gate, u = jnp.split(p, 3, axis=-1)
    z = c_gate * u
    T = z.shape[1]
    zp = jnp.pad(z, ((0, 0), (1, 1), (0, 0)))
    y = zp[:, 0:T] * conv_w[0] + zp[:, 1:T + 1] * conv_w[1] + zp[:, 2:T + 2] * conv_w[2]
    return b_gate * y


def even_mixer(h_lat, h_ctx, w_in, q_norm_g, kv_norm_g, w_uq, w_ukv, q_gain, k_gain, conv_w, w_o, with_ctx_out):
    B, S, _ = h_lat.shape
    L = h_ctx.shape[1]
    p_lat = h_lat @ w_in
    p_ctx = h_ctx @ w_in
    q_l, k_l, v_l = mla_qkv(p_lat, q_norm_g, kv_norm_g, w_uq, w_ukv, q_gain, k_gain, axial_angles(S))
    q_c, k_c, v_c = mla_qkv(p_ctx, q_norm_g, kv_norm_g, w_uq, w_ukv, q_gain, k_gain, None)
    k_all = jnp.concatenate([k_l, k_c], axis=1)
    v_all = jnp.concatenate([v_l, v_c], axis=1)
    a_l = attend_blocked(q_l, k_all, v_all)
    s_l = short_conv(p_lat[..., CONV_OFFSET:], conv_w)
    out_l = jnp.concatenate([a_l, s_l], axis=-1) @ w_o
    if not with_ctx_out:
        return out_l, None
    a_c = attend(q_c, k_c, v_c).reshape(B, L, MLA_HEADS * V_HEAD_DIM)
    s_c = short_conv(p_ctx[..., CONV_OFFSET:], conv_w)
    out_c = jnp.concatenate([a_c, s_c], axis=-1) @ w_o
    return out_l, out_c


def fourier_mixer(h, w_f):
    B, T, _ = h.shape
    hg = h.astype(jnp.float32).reshape(B, T, FOURIER_GROUPS, FOURIER_GROUP_DIM)
    f = jnp.fft.fft2(hg, axes=(1, 3), norm="ortho").real
    return f.reshape(B, T, D_MODEL).astype(h.dtype) @ w_f


def swiglu(h, w1, w3, w2):
    return (jax.nn.silu(h @ w1) * (h @ w3)) @ w2


def ada(cvec, w, b):
    return jnp.split(jax.nn.silu(cvec) @ w + b, 6, axis=-1)


def setup_inputs(seed: int = 0) -> dict:
    key = jax.random.key(seed)
    ks = jax.random.split(key, 24)
    nrm = jax.random.normal
    f32 = jnp.float32
    D = D_MODEL
    return {
        "x": nrm(ks[0], (BATCH, SEQ, D), f32),
        "c": nrm(ks[1], (BATCH, D), f32),
        "ctx": nrm(ks[2], (BATCH, CTX_LEN, D), f32),
        "c_ctx": nrm(ks[3], (D,), f32),
        "ada_w": nrm(ks[4], (DEPTH, D, 6 * D), f32) * D ** -0.5,
        "ada_b": nrm(ks[5], (DEPTH, 6 * D), f32) * 0.01,
        "norm1_g": 1.0 + 0.02 * nrm(ks[6], (DEPTH, D), f32),
        "norm2_g": 1.0 + 0.02 * nrm(ks[7], (DEPTH, D), f32),
        "w_in": nrm(ks[8], (N_EVEN, D, IN_PROJ_DIM), f32) * D ** -0.5,
        "q_norm_g": 1.0 + 0.02 * nrm(ks[9], (N_EVEN, Q_LORA_RANK), f32),
        "kv_norm_g": 1.0 + 0.02 * nrm(ks[10], (N_EVEN, KV_LORA_RANK), f32),
        "w_uq": nrm(ks[11], (N_EVEN, Q_LORA_RANK, MLA_HEADS * QK_HEAD_DIM), f32) * Q_LORA_RANK ** -0.5,
        "w_ukv": nrm(ks[12], (N_EVEN, KV_LORA_RANK, MLA_HEADS * (QK_NOPE_DIM + V_HEAD_DIM)), f32) * KV_LORA_RANK ** -0.5,
        "q_gain": 1.0 + 0.02 * nrm(ks[13], (N_EVEN, QK_HEAD_DIM), f32),
        "k_gain": 1.0 + 0.02 * nrm(ks[14], (N_EVEN, QK_HEAD_DIM), f32),
        "conv_w": nrm(ks[15], (N_EVEN, CONV_WIDTH, CONV_DIM), f32) * CONV_WIDTH ** -0.5,
        "w_o": nrm(ks[16], (N_EVEN, MIX_OUT_DIM, D), f32) * MIX_OUT_DIM ** -0.5,
        "w_fourier": nrm(ks[17], (N_ODD, D, D), f32) * D ** -0.5,
        "ffn_w1": nrm(ks[18], (DEPTH, D, D_FF), f32) * D ** -0.5,
        "ffn_w3": nrm(ks[19], (DEPTH, D, D_FF), f32) * D ** -0.5,
        "ffn_w2": nrm(ks[20], (DEPTH, D_FF, D), f32) * D_FF ** -0.5,
    }


def reference(x, c, ctx, c_ctx, ada_w, ada_b, norm1_g, norm2_g, w_in, q_norm_g, kv_norm_g, w_uq, w_ukv,
              q_gain, k_gain, conv_w, w_o, w_fourier, ffn_w1, ffn_w3, ffn_w2):
    for i in range(DEPTH):
        last = i == DEPTH - 1
        j = i // 2
        sh1, sc1, g1, sh2, sc2, g2 = [m[:, None, :] for m in ada(c, ada_w[i], ada_b[i])]
        csh1, csc1, cg1, csh2, csc2, cg2 = ada(c_ctx, ada_w[i], ada_b[i])
        h_l = rmsnorm(x, norm1_g[i]) * (1 + sc1) + sh1
        if i % 2 == 0:
            h_c = rmsnorm(ctx, norm1_g[i]) * (1 + csc1) + csh1
            out_l, out_c = even_mixer(h_l, h_c, w_in[j], q_norm_g[j], kv_norm_g[j], w_uq[j], w_ukv[j],
                                      q_gain[j], k_gain[j], conv_w[j], w_o[j], not last)
        else:
            out_l = fourier_mixer(h_l, w_fourier[j])
            out_c = None
            if not last:
                h_c = rmsnorm(ctx, norm1_g[i]) * (1 + csc1) + csh1
                out_c = fourier_mixer(h_c, w_fourier[j])
        x = x + g1 * out_l
        x = x + g2 * swiglu(rmsnorm(x, norm2_g[i]) * (1 + sc2) + sh2, ffn_w1[i], ffn_w3[i], ffn_w2[i])
        if not last:
            ctx = ctx + cg1 * out_c
            ctx = ctx + cg2 * swiglu(rmsnorm(ctx, norm2_g[i]) * (1 + csc2) + csh2, ffn_w1[i], ffn_w3[i], ffn_w2[i])
    return x
```

```python
import numpy as np
from contextlib import ExitStack
import concourse.bass as bass
import concourse.mybir as mybir
from concourse.bass_utils import run_bass_kernel_spmd

F32 = mybir.dt.float32
BF16 = mybir.dt.bfloat16
AF = mybir.ActivationFunctionType
ALU = mybir.AluOpType

D = 1024
S = 8192
NB = 2
LCTX = 256
NT = 2048
H = 8
DFF = 2816
NFF = 22
EPS = 1e-6
QK_SCALE = 96 ** -0.5
NCORES = 8

V_CS = 0
V_ADAB = 16
V_N1 = 112
V_N2 = 128
V_QNG = 144
V_KVNG = 147
V_QG = 149
V_KG = 150
V_CONV = 151
V_HM = 163
V_MB = 165
V_SEL = 167
NVEC = 175


class Sem:
    def __init__(self, h):
        self.h = h
        self.count = 0


class Prog:
    ENG = ["sync", "scalar", "vector", "gpsimd", "tensor"]

    def __init__(self, nc, es, ndma=24):
        self.nc = nc
        self.q = {e: [] for e in self.ENG}
        self.esem = {e: Sem(es.enter_context(nc.semaphore("s_" + e))) for e in ["scalar", "vector", "gpsimd", "tensor"]}
        self.dsems = [Sem(es.enter_context(nc.semaphore("d%d" % i))) for i in range(ndma)]
        self.dlast = [None] * ndma
        self.dnext = 0
        self.waited = {e: {} for e in self.ENG}
        self.lastw = {}
        self.reads = {}
        self.nps = 0
        self.groups = {}

    def _x(self, keys):
        out = []
        for k in keys:
            out.extend(self.groups.get(k, []))
            out.append(k)
        return out

    def _deps(self, reads, writes):
        reads, writes = self._x(reads), self._x(writes)
        toks = []
        for k in reads:
            if k in self.lastw:
                toks.append(self.lastw[k])
        for k in writes:
            if k in self.lastw:
                toks.append(self.lastw[k])
            toks.extend(self.reads.get(k, []))
        return toks

    def _commit(self, tok, reads, writes):
        reads, writes = self._x(reads), self._x(writes)
        for k in writes:
            self.lastw[k] = tok
            self.reads[k] = []
        for k in reads:
            if k not in writes:
                self.reads.setdefault(k, []).append(tok)

    def _waits(self, eng, toks):
        out = []
        for t in toks:
            if t is None:
                continue
            sem, val, src = t
            if src == eng and eng == "tensor":
                continue
            w = self.waited[eng]
            if w.get(id(sem), 0) >= val:
                continue
            w[id(sem)] = val
            out.append((sem, val))
        return out

    def op(self, eng, fn, reads=(), writes=(), extra=()):
        toks = self._deps(reads, writes) + list(extra)
        waits = self._waits(eng, toks)
        sem = self.esem[eng]
        sem.count += 1
        tok = (sem, sem.count, eng)
        self.q[eng].append((waits, fn, sem, 1))
        self._commit(tok, reads, writes)
        return tok

    def dma(self, eng, out, in_, reads=(), writes=(), extra=(), **kw):
        i = self.dnext % len(self.dsems)
        self.dnext += 1
        sem = self.dsems[i]
        toks = self._deps(reads, writes) + list(extra) + [self.dlast[i]]
        waits = self._waits(eng, toks)
        sem.count += 16
        tok = (sem, sem.count, "dma")
        self.dlast[i] = tok
        self.q[eng].append((waits, lambda e: e.dma_start(out=out, in_=in_, **kw), sem, 16))
        self._commit(tok, reads, writes)
        return tok

    def coll(self, ins_ap, outs_ap, reads, writes):
        if not hasattr(self, "cc"):
            raise RuntimeError("no cc sem")
        toks = self._deps(reads, writes) + [self.cc_last]
        waits = self._waits("gpsimd", toks)
        self.cc.count += 1
        tok = (self.cc, self.cc.count, "cc")
        self.cc_last = tok
        self.q["gpsimd"].append((waits, lambda e: e.collective_compute("AllGather", ALU.bypass, replica_groups=[list(range(NCORES))], ins=[ins_ap], outs=[outs_ap]), self.cc, 1))
        self._commit(tok, reads, writes)
        return tok

    def barrier(self):
        toks = [(sm, sm.count, name) for name, sm in self.esem.items() if sm.count > 0]
        toks += [t for t in self.dlast if t is not None]
        if self.cc_last is not None:
            toks.append(self.cc_last)
        for e in self.ENG:
            self.wait(e, toks)

    def wait(self, eng, toks):
        waits = self._waits(eng, toks)
        if waits:
            self.q[eng].append((waits, None, None, 0))

    def emit(self):
        nc = self.nc
        with nc.Block() as block:
            def mk(name):
                def body(e):
                    for waits, fn, sem, amt in self.q[name]:
                        for s, v in waits:
                            e.wait_ge(s.h, v)
                        if fn is not None:
                            ins = fn(e)
                            if sem is not None:
                                ins.then_inc(sem.h, amt)
                return body
            block.sync(mk("sync"))
            block.scalar(mk("scalar"))
            block.vector(mk("vector"))
            block.gpsimd(mk("gpsimd"))
            block.tensor(mk("tensor"))

    def mm(self, out, lhsT, rhs, start, stop, reads, writes):
        return self.op("tensor", lambda e: e.matmul(out, lhsT, rhs, start=start, stop=stop), reads, writes)

    def act(self, out, in_, func, reads, writes, eng="scalar", **kw):
        return self.op(eng, lambda e: e.activation(out=out, in_=in_, func=func, **kw), reads, writes)

    def tt(self, out, in0, in1, op, reads, writes, eng="vector"):
        return self.op(eng, lambda e: e.tensor_tensor(out=out, in0=in0, in1=in1, op=op), reads, writes)

    def stt(self, out, in0, scalar, in1, op0, op1, reads, writes, eng="vector"):
        if eng == "gpsimd":
            self.ts(in0, in0, scalar, None, op0, None, reads, [reads[0]], eng=eng)
            return self.tt(out, in0, in1, op1, reads, writes, eng=eng)
        return self.op(eng, lambda e: e.scalar_tensor_tensor(out=out, in0=in0, scalar=scalar, in1=in1, op0=op0, op1=op1), reads, writes)

    def ts(self, out, in0, s1, s2, op0, op1, reads, writes, eng="vector"):
        if s2 is None:
            return self.op(eng, lambda e: e.tensor_scalar(out=out, in0=in0, scalar1=s1, scalar2=None, op0=op0), reads, writes)
        return self.op(eng, lambda e: e.tensor_scalar(out=out, in0=in0, scalar1=s1, scalar2=s2, op0=op0, op1=op1), reads, writes)

    def copy(self, out, in_, reads, writes, eng="vector"):
        if eng == "scalar":
            return self.op(eng, lambda e: e.activation(out=out, in_=in_, func=AF.Copy), reads, writes)
        return self.op(eng, lambda e: e.tensor_copy(out=out, in_=in_), reads, writes)

    def memset(self, ap, val, writes, eng="vector"):
        return self.op(eng, lambda e: e.memset(ap, val), (), writes)

    def recip(self, out, in_, reads, writes):
        return self.op("vector", lambda e: e.reciprocal(out=out, in_=in_), reads, writes)


def build(debug=False, stop_after=None):
    nc = bass.Bass("TRN2", target_bir_lowering=False)

    def din(name, shape, dt=F32):
        return nc.dram_tensor(name, list(shape), dt, kind="ExternalInput").ap()

    def dout(name, shape, dt=F32):
        return nc.dram_tensor(name, list(shape), dt, kind="ExternalOutput").ap()

    x_d = din("x", [NT, D])
    xh_d = din("xh", [2, D])
    ctx_d = din("ctx", [LCTX, D])
    vecs_d = din("vecs", [128, NVEC])
    adaw_d = din("ada_w", [2, 12, 128, 8 * 512])
    win_d = din("w_in", [128, 8, 2208])
    wuq_d = din("w_uq", [128, 3, 768])
    wukv_d = din("w_ukv", [128, 2, 1024])
    wo_d = din("w_o", [8, 128, 8 * 128])
    wf_d = din("w_f", [8, 128, 8 * 128])
    w1_d = din("ffn_w1", [2, NFF, 128, 8 * 128])
    w3_d = din("ffn_w3", [2, NFF, 128, 8 * 128])
    w2_d = din("ffn_w2", [2, 8, 128, NFF * 128])
    ident_d = din("ident", [128, 128])
    ropec_d = din("ropec", [96, NT])
    ropes_d = din("ropes", [96, NT])
    pmT_d = din("pmT", [96, 96])
    shift_d = din("shiftm", [128, 128])
    fc_d = din("fcm", [128, 2, 4, 128])
    wa_d = din("wam", [128, 2, 256])
    mp_d = din("mpm", [64, 2, 128, 64])
    out_d = dout("out", [NT, D])

    q_dr = nc.dram_tensor("q_dr", [H, 96, NT], BF16).ap()
    s_dr = nc.dram_tensor("s_dr", [4, 128, NT + 1], BF16).ap()
    kc_dr = nc.dram_tensor("kc_dr", [H, 96, LCTX], BF16).ap()
    vc_dr = nc.dram_tensor("vc_dr", [LCTX, 512], BF16).ap()
    kv_in = nc.dram_tensor("kv_in", [1280, NT], BF16)
    kv_all = nc.dram_tensor("kv_all", [NCORES * 1280, NT], BF16)
    h1_in = nc.dram_tensor("h1_in", [D, NT], BF16)
    h1_all = nc.dram_tensor("h1_all", [NCORES * D, NT], BF16)
    f_in = nc.dram_tensor("f_in", [S, 256], BF16)
    f_all = nc.dram_tensor("f_all", [NCORES * S, 256], BF16)
    x1_dr = nc.dram_tensor("x1_dr", [8, 128, NT], F32).ap()

    dbg = {}
    if debug:
        dbg["q"] = dout("dbg_q", [H, 96, NT], BF16)
        dbg["kv"] = dout("dbg_kv", [1280, NT], BF16)
        dbg["s"] = dout("dbg_s", [4, 128, NT + 1], BF16)
        dbg["kc"] = dout("dbg_kc", [H, 96, LCTX], BF16)
        dbg["vc"] = dout("dbg_vc", [LCTX, 512], BF16)
        dbg["mod"] = dout("dbg_mod", [128, 2 * 96])
        dbg["x1"] = dout("dbg_x1", [8, 128, NT])
        dbg["h1"] = dout("dbg_h1", [D, NT], BF16)
        dbg["f"] = dout("dbg_f", [S, 256], BF16)
        dbg["a"] = dout("dbg_a", [4, 128, NT], BF16)

    with ExitStack() as es:
        P = Prog(nc, es)
        P.cc = Sem(es.enter_context(nc.semaphore("cc")))
        P.cc_last = None

        def sb(name, shape, dt=F32, stack=es):
            return stack.enter_context(nc.sbuf_tensor("t_" + name, list(shape), dt))

        psb = [es.enter_context(nc.psum_tensor("ps%d" % i, [128, 512], F32)) for i in range(8)]

        def psum():
            i = P.nps % 8
            P.nps += 1
            return psb[i], "ps%d" % i

        wst = [sb("wst%d" % i, [128, 4096]) for i in range(2)]
        nst = [0]

        def load_cast(dst, src_d, ncols, dkey):
            i = nst[0] % 2
            nst[0] += 1
            Pn = dst.shape[0]
            P.dma("sync", wst[i][0:Pn, 0:ncols], src_d, writes=["wst%d" % i])
            if ncols < 512:
                P.copy(dst, wst[i][0:Pn, 0:ncols], ["wst%d" % i], [dkey], eng="vector")
            else:
                P.groups[dkey] = [dkey + "_a", dkey + "_b", dkey + "_c"]
                c1 = (ncols * 14 // 100) // 8 * 8
                c2 = c1 + (ncols * 46 // 100) // 8 * 8
                P.copy(dst[:, 0:c1], wst[i][0:Pn, 0:c1], ["wst%d" % i], [dkey + "_a"], eng="gpsimd")
                P.copy(dst[:, c1:c2], wst[i][0:Pn, c1:c2], ["wst%d" % i], [dkey + "_b"], eng="vector")
                P.copy(dst[:, c2:ncols], wst[i][0:Pn, c2:ncols], ["wst%d" % i], [dkey + "_c"], eng="scalar")

        vecs = sb("vecs", [128, NVEC])
        ident = sb("ident", [128, 128])
        identb = sb("identb", [128, 128], BF16)
        onesb = sb("onesb", [128, 128], BF16)
        mod = sb("mod", [128, 2, 48, 2])
        A1 = sb("A1", [128, 2, 8, 2])
        A2 = sb("A2", [128, 2, 8, 2])
        gsc = sb("gsc", [128, 2])
        P.dma("sync", vecs[:], vecs_d, writes=["vecs"])
        P.dma("sync", ident[:], ident_d, writes=["ident"])
        P.copy(identb[:], ident[:], ["ident"], ["identb"])
        P.memset(onesb[:], 1.0, ["onesb"])

        def pnorm(chunks, keys, nfeat, N, tmp_pool):
            Pi = chunks[0].shape[0]
            sq, rs = tmp_pool
            ps, pk = psum()
            for i, (c, k) in enumerate(zip(chunks, keys)):
                sqi = sq[0:Pi, i % 2, 0:N]
                P.act(sqi, c, AF.Square, [k], ["sq%d" % (i % 2)])
                P.mm(ps[0:Pi, 0:N], onesb[0:Pi, 0:Pi], sqi, i == 0, i == len(chunks) - 1, ["sq%d" % (i % 2), "onesb"], [pk])
            P.act(rs[0:Pi, 0:N], ps[0:Pi, 0:N], AF.Sqrt, [pk], ["rs"], scale=1.0 / nfeat, bias=EPS)
            P.recip(rs[0:Pi, 0:N], rs[0:Pi, 0:N], ["rs"], ["rs"])
            return rs[0:Pi, 0:N], "rs"

        with ExitStack() as s0:
            csb = sb("csb", [128, 16], BF16, s0)
            wts = [sb("adawt%d" % i, [128, 8, 512], BF16, s0) for i in range(2)]
            P.act(csb[:], vecs[:, V_CS:V_CS + 16], AF.Silu, ["vecs"], ["csb"])
            n = 0
            for l in range(2):
                ps, pk = psum()
                for nb in range(12):
                    wt = wts[n % 2]
                    wk_ = "adawt%d" % (n % 2)
                    n += 1
                    load_cast(wt[:].rearrange("p a b -> p (a b)"), adaw_d[l, nb], 4096, wk_)
                    for jj in range(4):
                        j = nb * 4 + jj
                        for kc in range(8):
                            P.mm(ps[:, 2 * j:2 * j + 2], wt[:, kc, jj * 128:(jj + 1) * 128], csb[:, 2 * kc:2 * kc + 2],
                                 kc == 0, kc == 7, [wk_, "csb"], [pk])
                P.tt(mod[:, l], ps[:, 0:96].rearrange("p (a b) -> p a b", b=2),
                     vecs[:, V_ADAB + 48 * l:V_ADAB + 48 * (l + 1)].unsqueeze(2).to_broadcast([128, 48, 2]),
                     ALU.add, [pk, "vecs"], ["mod"])
                P.stt(A1[:, l], mod[:, l, 8:16, :], 1.0, vecs[:, V_N1 + 8 * l:V_N1 + 8 * (l + 1)].unsqueeze(2).to_broadcast([128, 8, 2]),
                      ALU.add, ALU.mult, ["mod", "vecs"], ["A1"])
                P.stt(A2[:, l], mod[:, l, 32:40, :], 1.0, vecs[:, V_N2 + 8 * l:V_N2 + 8 * (l + 1)].unsqueeze(2).to_broadcast([128, 8, 2]),
                      ALU.add, ALU.mult, ["mod", "vecs"], ["A2"])
            P.act(gsc[:, 0:1], vecs[:, V_QG:V_QG + 1], AF.Copy, ["vecs"], ["gsc"], scale=QK_SCALE)
            P.act(gsc[:, 1:2], vecs[:, V_KG:V_KG + 1], AF.Copy, ["vecs"], ["gsc"])
            if debug:
                P.dma("sync", dbg["mod"], mod[:].rearrange("p l a b -> p (l a b)"), reads=["mod"])
        P.barrier()
        def mv(l, idx, fc, j):
            return mod[:, l, idx * 8 + fc, j:j + 1]

        with ExitStack() as sa:
            winb = sb("winb", [128, 8, 2208], BF16, sa)
            wpe = sb("wpe", [128, 8, 96], BF16, sa)
            wuqb = sb("wuqb", [128, 3, 768], BF16, sa)
            wukvb = sb("wukvb", [128, 2, 1024], BF16, sa)
            wk = sb("wk", [128, 2, 8, 96], BF16, sa)
            wv = sb("wv", [128, 2, 8, 64], BF16, sa)
            ropec = sb("ropec", [96, NT], F32, sa)
            ropes = sb("ropes", [96, NT], F32, sa)
            pmT = sb("pmT", [96, 96], BF16, sa)
            xTb = sb("xTb", [128, 8, 512], F32, sa)
            stage = sb("stage", [128, D], F32, sa)
            hT = sb("hT", [128, 8, 512], BF16, sa)
            sq = sb("sq", [128, 2, 512], BF16, sa)
            rs = sb("rs", [128, 512], F32, sa)
            tmpf = sb("tmpf", [128, 512], F32, sa)
            cqf = sb("cqf", [128, 5, 512], F32, sa)
            cqn = sb("cqn", [128, 5, 512], BF16, sa)
            qn = sb("qn", [96, 512], BF16, sa)
            t1 = sb("t1", [96, 512], F32, sa)
            t2 = sb("t2", [96, 512], F32, sa)
            qo = sb("qo", [96, 2, 512], BF16, sa)
            vst = sb("vst", [128, 2, 512], BF16, sa)
            zb = sb("zb", [128, 4, 514], F32, sa)
            bb = sb("bb", [128, 4, 513], F32, sa)
            uf = sb("uf", [128, 512], F32, sa)
            yt = sb("yt", [128, 512], F32, sa)
            so = sb("so", [128, 2, 512], BF16, sa)

            for kc in range(8):
                load_cast(winb[:, kc, :], win_d[:, kc, :], 2208, "winb")
            load_cast(wuqb[:].rearrange("p a b -> p (a b)"), wuq_d.rearrange("p a b -> p (a b)"), 3 * 768, "wuqb")
            load_cast(wukvb[:].rearrange("p a b -> p (a b)"), wukv_d.rearrange("p a b -> p (a b)"), 2048, "wukvb")
            load_cast(pmT[:], pmT_d, 96, "pmT")
            P.dma("sync", ropec[:], ropec_d, writes=["ropec"])
            P.dma("sync", ropes[:], ropes_d, writes=["ropes"])
            P.memset(wpe[:], 0.0, ["wpe"])
            P.memset(wk[:], 0.0, ["wk"])
            P.memset(zb[:], 0.0, ["zb"])
            P.memset(bb[:], 0.0, ["bb"])
            P.copy(wpe[:, :, 64:96], winb[:, :, 640:672], ["winb"], ["wpe"])
            wukv4 = wukvb[:].rearrange("p k (h d) -> p k h d", d=128)
            P.copy(wk[:, :, :, 0:64], wukv4[:, :, :, 0:64], ["wukvb"], ["wk"])
            P.copy(wv[:], wukv4[:, :, :, 64:128], ["wukvb"], ["wv"])

            nq = [0]

            def load_T(rows_ap, ntok, col0):
                P.dma("sync", stage[0:ntok, :], rows_ap, writes=["stage"])
                for g in range(2):
                    ps, pk = psum()
                    for f4 in range(4):
                        fc = g * 4 + f4
                        P.op("tensor", lambda e, fc=fc, f4=f4, ps=ps: e.transpose(ps[:, f4 * 128:f4 * 128 + ntok], stage[0:ntok, fc * 128:(fc + 1) * 128], ident[0:ntok, 0:ntok]),
                             ["stage", "ident"], [pk])
                    P.copy(xTb[:, g * 4:(g + 1) * 4, col0:col0 + ntok],
                           ps[:].rearrange("p (f t) -> p f t", t=128)[:, :, 0:ntok], [pk], ["xTb"], eng="scalar")

            def rms_mod(src, srck, N, Asc, Bsh, dst, dstk):
                rstd, rk = pnorm([src[:, fc, 0:N] for fc in range(8)], [srck] * 8, D, N, (sq, rs))
                for fc in range(8):
                    P.stt(tmpf[:, 0:N], src[:, fc, 0:N], Asc(fc), rstd, ALU.mult, ALU.mult, [srck, rk, "A1", "A2"], ["tmpf"])
                    P.act(dst[:, fc, 0:N], tmpf[:, 0:N], AF.Identity, ["tmpf", "mod"], [dstk], bias=Bsh(fc))

            def lin_in(col0, ncols, N, wsrc=None):
                ps, pk = psum()
                for kc in range(8):
                    P.mm(ps[0:ncols, 0:N], winb[:, kc, col0:col0 + ncols], hT[:, kc, 0:N], kc == 0, kc == 7, ["winb", "hT"], [pk])
                return ps, pk

            def lora_norm(col0, nch, gcol, N, base):
                for i in range(nch):
                    ps, pk = lin_in(col0 + 128 * i, 128, N)
                    P.copy(cqf[:, base + i, 0:N], ps[:, 0:N], [pk], ["cqf%d" % base], eng="scalar")
                rstd, rk = pnorm([cqf[:, base + i, 0:N] for i in range(nch)], ["cqf%d" % base] * nch, nch * 128, N, (sq, rs))
                for i in range(nch):
                    P.stt(cqn[:, base + i, 0:N], cqf[:, base + i, 0:N], vecs[:, gcol + i:gcol + i + 1], rstd, ALU.mult, ALU.mult,
                          ["cqf%d" % base, rk, "vecs"], ["cqn%d" % base])

            def head_norm_rope(ps, pk, N, gcol, tok0, rope, out_ap, outk):
                P.copy(t1[:, 0:N], ps[0:96, 0:N], [pk], ["t1"], eng="scalar")
                rstd, rk = pnorm([t1[:, 0:N]], ["t1"], 96, N, (sq, rs))
                if not rope:
                    P.stt(out_ap, t1[:, 0:N], gsc[0:96, gcol:gcol + 1], rstd, ALU.mult, ALU.mult, ["t1", rk, "gsc"], [outk])
                    return
                P.stt(qn[:, 0:N], t1[:, 0:N], gsc[0:96, gcol:gcol + 1], rstd, ALU.mult, ALU.mult, ["t1", rk, "gsc"], ["qn"])
                ps2, pk2 = psum()
                P.mm(ps2[0:96, 0:N], pmT[:], qn[:, 0:N], True, True, ["pmT", "qn"], [pk2])
                P.tt(t2[:, 0:N], ps2[0:96, 0:N], ropes[:, tok0:tok0 + N], ALU.mult, [pk2, "ropes"], ["t2"])
                P.tt(t1[:, 0:N], qn[:, 0:N], ropec[:, tok0:tok0 + N], ALU.mult, ["qn", "ropec"], ["t1"])
                P.tt(out_ap, t1[:, 0:N], t2[:, 0:N], ALU.add, ["t1", "t2"], [outk])

            def proc(kind, N, tok0):
                j = 1 if kind == "ctx" else 0
                rms_mod(xTb, "xTb", N, lambda fc: A1[:, 0, fc, j:j + 1], lambda fc: mv(0, 0, fc, j), hT, "hT")
                if kind == "lat":
                    lora_norm(0, 3, V_QNG, N, 0)
                    for h in range(H):
                        ps, pk = psum()
                        for kc in range(3):
                            P.mm(ps[0:96, 0:N], wuqb[:, kc, h * 96:(h + 1) * 96], cqn[:, kc, 0:N], kc == 0, kc == 2, ["wuqb", "cqn0"], [pk])
                        o = nq[0] % 2
                        nq[0] += 1
                        head_norm_rope(ps, pk, N, 0, tok0, True, qo[:, o, 0:N], "qo%d" % o)
                        P.dma("sync", q_dr[h, :, tok0:tok0 + N], qo[:, o, 0:N], reads=["qo%d" % o], writes=["q_dr"])
                if kind in ("lat", "ctx"):
                    lora_norm(384, 2, V_KVNG, N, 3)
                    for h in range(H):
                        ps, pk = psum()
                        for kc in range(2):
                            P.mm(ps[0:96, 0:N], wk[:, kc, h, :], cqn[:, 3 + kc, 0:N], kc == 0, False, ["wk", "cqn3"], [pk])
                        for kc in range(8):
                            P.mm(ps[0:96, 0:N], wpe[:, kc, :], hT[:, kc, 0:N], False, kc == 7, ["wpe", "hT"], [pk])
                        o = nq[0] % 2
                        nq[0] += 1
                        head_norm_rope(ps, pk, N, 1, tok0, kind == "lat", qo[:, o, 0:N], "qo%d" % o)
                        if kind == "lat":
                            P.dma("sync", kv_in.ap()[h * 96:(h + 1) * 96, tok0:tok0 + N], qo[:, o, 0:N], reads=["qo%d" % o], writes=["kv_in"])
                        else:
                            P.dma("sync", kc_dr[h, :, :], qo[:, o, 0:N], reads=["qo%d" % o], writes=["kc_dr"])
                    for tt_ in range(N // 128):
                        ps, pk = psum()
                        for kc in range(2):
                            P.mm(ps[:, :], cqn[:, 3 + kc, tt_ * 128:(tt_ + 1) * 128], wv[:, kc].rearrange("p h d -> p (h d)"),
                                 kc == 0, kc == 1, ["wv", "cqn3"], [pk])
                        o = nq[0] % 2
                        nq[0] += 1
                        P.copy(vst[:, o, :], ps[:, :], [pk], ["vst%d" % o], eng="scalar")
                        if kind == "lat":
                            t0 = tok0 + tt_ * 128
                            dst = kv_in.ap()[768 + t0 // 4:768 + t0 // 4 + 32, :].rearrange("r (q c) -> (r q) c", c=512)
                            P.dma("sync", dst, vst[:, o, :], reads=["vst%d" % o], writes=["kv_in"])
                        else:
                            P.dma("sync", vc_dr[tt_ * 128:(tt_ + 1) * 128, :], vst[:, o, :], reads=["vst%d" % o], writes=["vc_dr"])
                if kind == "lat":
                    for ch in range(4):
                        psb_, pkb = lin_in(672 + 128 * ch, 128, N)
                        P.copy(bb[:, ch, 1:1 + N], psb_[:, 0:N], [pkb], ["bb"], eng="scalar")
                        psu, pku = lin_in(1696 + 128 * ch, 128, N)
                        P.copy(uf[:, 0:N], psu[:, 0:N], [pku], ["uf"], eng="scalar")
                        psc, pkc = lin_in(1184 + 128 * ch, 128, N)
                        P.tt(zb[:, ch, 2:2 + N], psc[:, 0:N], uf[:, 0:N], ALU.mult, [pkc, "uf"], ["zb"])
                    conv_out(N, tok0)
                if kind == "halo":
                    for ch in range(4):
                        psu, pku = lin_in(1696 + 128 * ch, 128, N)
                        P.copy(uf[:, 0:N], psu[:, 0:N], [pku], ["uf"], eng="scalar")
                        psc, pkc = lin_in(1184 + 128 * ch, 128, N)
                        P.tt(uf[:, 0:N], psc[:, 0:N], uf[:, 0:N], ALU.mult, [pkc, "uf"], ["uf"])
                        P.tt(zh[:, ch, :], uf[:, 0:2], vecs[:, V_HM:V_HM + 2], ALU.mult, ["uf", "vecs"], ["zh"])

            def conv_out(N, tok0):
                for ch in range(4):
                    P.ts(yt[:, 0:N], zb[:, ch, 0:N], vecs[:, V_CONV + ch:V_CONV + ch + 1], None, ALU.mult, ALU.bypass, ["zb", "vecs"], ["yt"])
                    P.stt(yt[:, 0:N], zb[:, ch, 1:1 + N], vecs[:, V_CONV + 4 + ch:V_CONV + 5 + ch], yt[:, 0:N], ALU.mult, ALU.add, ["zb", "vecs", "yt"], ["yt"])
                    P.stt(yt[:, 0:N], zb[:, ch, 2:2 + N], vecs[:, V_CONV + 8 + ch:V_CONV + 9 + ch], yt[:, 0:N], ALU.mult, ALU.add, ["zb", "vecs", "yt"], ["yt"])
                    o = nq[0] % 2
                    nq[0] += 1
                    P.tt(so[:, o, 0:N], yt[:, 0:N], bb[:, ch, 0:N], ALU.mult, ["yt", "bb"], ["so%d" % o])
                    P.dma("sync", s_dr[ch, :, tok0:tok0 + N], so[:, o, 0:N], reads=["so%d" % o], writes=["s_dr"], allow_slow_non_contiguous=True)
                P.copy(zb[:, :, 0:2], zb[:, :, N:N + 2], ["zb"], ["zb"])
                P.copy(bb[:, :, 0:1], bb[:, :, N:N + 1], ["bb"], ["bb"])

            zh = sb("zh", [128, 4, 2], F32, sa)
            load_T(xh_d, 2, 0)
            proc("halo", 2, 0)
            P.copy(zb[:, :, 1:2], zh[:, :, 0:1], ["zh", "zb"], ["zb"])
            for tt_ in range(2):
                load_T(ctx_d[tt_ * 128:(tt_ + 1) * 128, :], 128, tt_ * 128)
            proc("ctx", 256, 0)
            for blk in range(4):
                for tt_ in range(4):
                    load_T(x_d[blk * 512 + tt_ * 128:blk * 512 + (tt_ + 1) * 128, :], 128, tt_ * 128)
                proc("lat", 512, blk * 512)
            P.copy(zb[:, :, 2:3], zh[:, :, 1:2], ["zh", "zb"], ["zb"])
            conv_out(1, NT)

            if debug:
                P.dma("sync", dbg["q"], q_dr, reads=["q_dr"])
                P.dma("sync", dbg["kv"], kv_in.ap(), reads=["kv_in"])
                P.dma("sync", dbg["s"], s_dr, reads=["s_dr"])
                P.dma("sync", dbg["kc"], kc_dr, reads=["kc_dr"])
                P.dma("sync", dbg["vc"], vc_dr, reads=["vc_dr"])


        P.barrier()

        def psr(lo, hi, ctr):
            i = lo + ctr[0] % (hi - lo)
            ctr[0] += 1
            return psb[i], "ps%d" % i

        a_dr = nc.dram_tensor("a_dr", [4, 128, NT], BF16).ap()
        P.coll(kv_in.ap(), kv_all.ap(), ["kv_in"], ["kv_all"])
        kva = kv_all.ap()

        with ExitStack() as sB:
            kT = [sb("kT%d" % i, [96, S + LCTX], BF16, sB) for i in range(2)]
            vaug = [sb("vaug%d" % i, [128, 66, 128], BF16, sB) for i in range(2)]
            kst = [sb("kst%d" % i, [96, 2, NT], BF16, sB) for i in range(2)]
            vsg = [sb("vsg%d" % i, [128, 2, 16, 64], BF16, sB) for i in range(2)]
            qh = [sb("qh%d" % i, [96, NT], BF16, sB) for i in range(2)]
            pT = [sb("pT%d" % i, [128, 512], BF16, sB) for i in range(4)]
            OS = [sb("OS%d" % i, [128, 512], F32, sB) for i in range(2)]
            Rr = sb("Rr", [128, 512], F32, sB)
            ao = [sb("ao%d" % i, [128, 512], BF16, sB) for i in range(2)]
            shiftm = sb("shiftm", [128, 128], F32, sB)
            P.dma("sync", shiftm[:], shift_d, writes=["shiftm"])
            for i in range(2):
                P.memset(vaug[i][:], 1.0, ["vaug%d" % i], eng="gpsimd")
            m0 = vecs[:, V_MB:V_MB + 1]
            m1 = vecs[:, V_MB + 1:V_MB + 2]
            nsel = [0]
            sctr = [0]
            octr = [0]
            zer = sb("zer", [128, NT], BF16, sB)
            P.memset(zer[:], 0.0, ["zer"])

            def load_head(h):
                hb = h % 2
                off = 0 if hb == 0 else 64
                kk, vk, qk = "kT%d" % hb, "vaug%d" % hb, "qh%d" % hb
                P.dma("sync", qh[hb][:], q_dr[h], reads=["q_dr"], writes=[qk])
                for r in range(4):
                    i = nsel[0] % 2
                    nsel[0] += 1
                    for cb in range(2):
                        R = 4 * cb + r
                        P.dma("sync", kst[i][:, cb, :], kva[R * 1280 + h * 96:R * 1280 + (h + 1) * 96, :], reads=["kv_all"], writes=["kst%d_%d" % (i, cb)])
                        vsrc = kva[R * 1280 + 768:R * 1280 + 1280, :].rearrange("r (q c) -> (r q) c", c=512).rearrange("(t p) c -> p t c", p=128)[:, :, h * 64:(h + 1) * 64]
                        P.dma("sync", vsg[i][:, cb], vsrc, reads=["kv_all"], writes=["vsg%d_%d" % (i, cb)])
                    kdst = kT[hb][:, r * NT:(r + 1) * NT]
                    P.stt(kdst, kst[i][:, 0, :], m0[0:96], zer[0:96, :], ALU.mult, ALU.add, ["kst%d_0" % i, "vecs", "zer"], [kk])
                    P.stt(kdst, kst[i][:, 1, :], m1[0:96], kdst, ALU.mult, ALU.add, ["kst%d_1" % i, "vecs", kk], [kk])
                    vdst = vaug[hb][:, r * 16:(r + 1) * 16, off:off + 64]
                    zv = zer[:, 0:1024].rearrange("p (t c) -> p t c", c=64)
                    P.stt(vdst, vsg[i][:, 0], m0, zv, ALU.mult, ALU.add, ["vsg%d_0" % i, "vecs", "zer"], [vk])
                    P.stt(vdst, vsg[i][:, 1], m1, vdst, ALU.mult, ALU.add, ["vsg%d_1" % i, "vecs", vk], [vk])
                P.dma("sync", kT[hb][:, S:S + LCTX], kc_dr[h], reads=["kc_dr"], writes=[kk])
                P.dma("sync", vaug[hb][:, 64:66, off:off + 64], vc_dr.rearrange("(t p) c -> p t c", p=128)[:, :, h * 64:(h + 1) * 64], reads=["vc_dr"], writes=[vk])

            def compute_head(h):
                hb = h % 2
                kk, vk, qk = "kT%d" % hb, "vaug%d" % hb, "qh%d" % hb
                for qb in range(4):
                    po, pok = psr(6, 8, octr)
                    NK = 66
                    sb_list = []

                    def s_mm(kt):
                        ps_, psk = psr(0, 6, sctr)
                        P.mm(ps_[:, :], kT[hb][:, kt * 128:(kt + 1) * 128], qh[hb][:, qb * 512:(qb + 1) * 512], True, True, [kk, qk], [psk])
                        sb_list.append((ps_, psk))
                    s_mm(0)
                    s_mm(1)
                    for kt in range(NK):
                        if kt + 2 < NK:
                            s_mm(kt + 2)
                        ps_, psk = sb_list[kt]
                        pi = kt % 4
                        P.act(pT[pi][:], ps_[:, :], AF.Exp, [psk], ["pT%d" % pi])
                        P.mm(po[:, :], vaug[hb][:, kt, :], pT[pi][:], kt == 0, kt == NK - 1, [vk, "pT%d" % pi], [pok])
                    o = (h * 4 + qb) % 2
                    P.copy(OS[o][:], po[:, :], [pok], ["OS%d" % o], eng="scalar")
                    src = slice(64, 128) if hb == 0 else slice(0, 64)
                    dst = slice(0, 64) if hb == 0 else slice(64, 128)
                    P.recip(Rr[src, :], OS[o][src, :], ["OS%d" % o], ["Rr"])
                    pn, pnk = psr(0, 6, sctr)
                    P.mm(pn[:, :], shiftm[src, :], Rr[src, :], True, True, ["shiftm", "Rr"], [pnk])
                    c = h // 2
                    P.tt(ao[hb][dst, :], OS[o][dst, :], pn[dst, :], ALU.mult, ["OS%d" % o, pnk], ["ao%d" % hb])
                    P.dma("sync", a_dr[c, dst, qb * 512:(qb + 1) * 512], ao[hb][dst, :], reads=["ao%d" % hb], writes=["a_dr"])
            for h in range(H):
                load_head(h)
                compute_head(h)
            if debug:
                P.dma("sync", dbg["a"], a_dr, reads=["a_dr"])

        P.barrier()
        FSB = 512

        def ffn(l, xT, sF):
            hT = sb("f_hT%d" % l, [128, 8, FSB], BF16, sF)
            AT = sb("f_AT%d" % l, [128, NFF, FSB], BF16, sF)
            w1b = [sb("f_w1b%d_%d" % (l, i), [128, 8, 128], BF16, sF) for i in range(2)]
            w3b = [sb("f_w3b%d_%d" % (l, i), [128, 8, 128], BF16, sF) for i in range(2)]
            w2b = [sb("f_w2b%d_%d" % (l, i), [128, NFF, 128], BF16, sF) for i in range(2)]
            sg = sb("f_sg%d" % l, [128, 512], F32, sF)
            sq = sb("f_sq%d" % l, [128, 2, 512], BF16, sF)
            rs = sb("f_rs%d" % l, [128, 512], F32, sF)
            tmpf = sb("f_tmpf%d" % l, [128, 512], F32, sF)
            n = 0
            for sbk in range(NT // FSB):
                t0 = sbk * FSB
                rstd, rk = pnorm([xT[:, fc, t0:t0 + FSB] for fc in range(8)], ["xT"] * 8, D, FSB, (sq, rs))
                for fc in range(8):
                    P.stt(tmpf[:], xT[:, fc, t0:t0 + FSB], A2[:, l, fc, 0:1], rstd, ALU.mult, ALU.mult, ["xT", rk, "A2"], ["f_tmpf"])
                    P.act(hT[:, fc, :], tmpf[:], AF.Identity, ["f_tmpf", "mod"], ["f_hT"], bias=mv(l, 3, fc, 0))
                for jf in range(NFF):
                    i = n % 2
                    n += 1
                    load_cast(w1b[i][:].rearrange("p a b -> p (a b)"), w1_d[l, jf], 1024, "f_w1b%d" % i)
                    load_cast(w3b[i][:].rearrange("p a b -> p (a b)"), w3_d[l, jf], 1024, "f_w3b%d" % i)
                    pg, pgk = psum()
                    for kc in range(8):
                        P.mm(pg[:, 0:FSB], w1b[i][:, kc, :], hT[:, kc, :], kc == 0, kc == 7, ["f_w1b%d" % i, "f_hT"], [pgk])
                    pu, puk = psum()
                    for kc in range(8):
                        P.mm(pu[:, 0:FSB], w3b[i][:, kc, :], hT[:, kc, :], kc == 0, kc == 7, ["f_w3b%d" % i, "f_hT"], [puk])
                    P.act(sg[:, 0:FSB], pg[:, 0:FSB], AF.Silu, [pgk], ["f_sg"])
                    P.tt(AT[:, jf, :], sg[:, 0:FSB], pu[:, 0:FSB], ALU.mult, ["f_sg", puk], ["f_AT"])
                for m in range(8):
                    i = n % 2
                    n += 1
                    load_cast(w2b[i][:].rearrange("p a b -> p (a b)"), w2_d[l, m], NFF * 128, "f_w2b%d" % i)
                    po, pok = psum()
                    for jf in range(NFF):
                        P.mm(po[:, 0:FSB], w2b[i][:, jf, :], AT[:, jf, :], jf == 0, jf == NFF - 1, ["f_w2b%d" % i, "f_AT"], [pok])
                    P.stt(xT[:, m, t0:t0 + FSB], po[:, 0:FSB], mv(l, 5, m, 0), xT[:, m, t0:t0 + FSB], ALU.mult, ALU.add, [pok, "mod", "xT"], ["xT"])

        def proj_res(l, wd, srcT, srck, xT, sP):
            wb = [sb("p_wb%d_%d" % (l, i), [128, 8, 128], BF16, sP) for i in range(2)]
            for m in range(8):
                i = m % 2
                load_cast(wb[i][:].rearrange("p a b -> p (a b)"), wd[m], 1024, "p_wb%d" % i)
                for blk in range(4):
                    po, pok = psum()
                    for kc in range(8):
                        P.mm(po[:, :], wb[i][:, kc, :], srcT[:, kc, blk * 512:(blk + 1) * 512], kc == 0, kc == 7, ["p_wb%d" % i, srck], [pok])
                    P.stt(xT[:, m, blk * 512:(blk + 1) * 512], po[:, :], mv(l, 2, m, 0), xT[:, m, blk * 512:(blk + 1) * 512], ALU.mult, ALU.add, [pok, "mod", "xT"], ["xT"])

        def load_xT(xT, rows_of_tile, sX):
            stage = sb("x_stage%d" % rows_of_tile[1], [128, D], F32, sX)
            for tt_ in range(NT // 128):
                P.dma("sync", stage[:], rows_of_tile[0](tt_), writes=["x_stage"])
                for g in range(2):
                    ps, pk = psum()
                    for f4 in range(4):
                        fc = g * 4 + f4
                        P.op("tensor", lambda e, fc=fc, f4=f4, ps=ps: e.transpose(ps[:, f4 * 128:(f4 + 1) * 128], stage[:, fc * 128:(fc + 1) * 128], ident[:]),
                             ["x_stage", "ident"], [pk])
                    P.copy(xT[:, g * 4:(g + 1) * 4, tt_ * 128:(tt_ + 1) * 128], ps[:].rearrange("p (f t) -> p f t", t=128), [pk], ["xT"], eng="scalar")

        with ExitStack() as sC:
            xT = sb("xT", [128, 8, NT], F32, sC)
            with ExitStack() as sC1:
                load_xT(xT, (lambda tt_: x_d[tt_ * 128:(tt_ + 1) * 128, :], 0), sC1)
                mixT = sb("mixT", [128, 8, NT], BF16, sC1)
                for c in range(4):
                    P.dma("sync", mixT[:, c, :], a_dr[c], reads=["a_dr"], writes=["mixT"])
                    P.dma("sync", mixT[:, 4 + c, :], s_dr[c, :, 1:NT + 1], reads=["s_dr"], writes=["mixT"])
                proj_res(0, wo_d, mixT, "mixT", xT, sC1)
            P.barrier()
            with ExitStack() as sC2:
                ffn(0, xT, sC2)
                sq = sb("c_sq", [128, 2, 512], BF16, sC2)
                rs = sb("c_rs", [128, 512], F32, sC2)
                tmpf = sb("c_tmpf", [128, 512], F32, sC2)
                ho = [sb("c_ho%d" % i, [128, 512], BF16, sC2) for i in range(2)]
                for blk in range(4):
                    t0 = blk * 512
                    rstd, rk = pnorm([xT[:, fc, t0:t0 + 512] for fc in range(8)], ["xT"] * 8, D, 512, (sq, rs))
                    for fc in range(8):
                        o = fc % 2
                        P.stt(tmpf[:], xT[:, fc, t0:t0 + 512], A1[:, 1, fc, 0:1], rstd, ALU.mult, ALU.mult, ["xT", rk, "A1"], ["c_tmpf"])
                        P.act(ho[o][:], tmpf[:], AF.Identity, ["c_tmpf", "mod"], ["c_ho%d" % o], bias=mv(1, 0, fc, 0))
                        P.dma("sync", h1_in.ap()[fc * 128:(fc + 1) * 128, t0:t0 + 512], ho[o][:], reads=["c_ho%d" % o], writes=["h1_in"])
                for fc in range(8):
                    P.dma("sync", x1_dr[fc], xT[:, fc, :], reads=["xT"], writes=["x1_dr"])
            if debug:
                P.dma("sync", dbg["x1"], x1_dr, reads=["x1_dr"])
                P.dma("sync", dbg["h1"], h1_in.ap(), reads=["h1_in"])
        P.barrier()
        P.coll(h1_in.ap(), h1_all.ap(), ["h1_in"], ["h1_all"])
        h1a = h1_all.ap()

        with ExitStack() as sD:
            XT = sb("XT", [128, 2, S], BF16, sD)
            cst = [sb("d_cst%d" % i, [128, NT], BF16, sD) for i in range(4)]
            dzer = sb("dzer", [128, NT], BF16, sD)
            P.memset(dzer[:], 0.0, ["dzer"])
            fcb = sb("fcb", [128, 2, 4, 128], BF16, sD)
            wab = sb("wab", [128, 2, 256], BF16, sD)
            mpb = sb("mpb", [64, 2, 128, 64], BF16, sD)
            Zsb = sb("Zsb", [128, 64, 128], BF16, sD)
            Ysb = sb("Ysb", [64, 64, 256], BF16, sD)
            Fsb = sb("Fsb", [64, 128, 64], BF16, sD)
            load_cast(fcb[:].rearrange("p a b c -> p (a b c)"), fc_d.rearrange("p a b c -> p (a b c)"), 1024, "fcb")
            load_cast(wab[:].rearrange("p a b -> p (a b)"), wa_d.rearrange("p a b -> p (a b)"), 512, "wab")
            mpf = mpb[:].rearrange("p a b c -> p (a b c)")
            mpd = mp_d.rearrange("p a b c -> p (a b c)")
            for i4 in range(4):
                load_cast(mpf[:, i4 * 4096:(i4 + 1) * 4096], mpd[:, i4 * 4096:(i4 + 1) * 4096], 4096, "mpb")
            nsel = 0
            for r in range(4):
                for cc in range(2):
                    dstX = XT[:, cc, r * NT:(r + 1) * NT]
                    for k in range(8):
                        bb_, gg = k // 4, k % 4
                        i = nsel % 4
                        nsel += 1
                        R = 4 * bb_ + r
                        row0 = R * D + gg * 256 + cc * 128
                        P.dma("sync", cst[i][:], h1a[row0:row0 + 128, :], reads=["h1_all"], writes=["d_cst%d" % i])
                        sel = vecs[:, V_SEL + k:V_SEL + k + 1]
                        P.stt(dstX, cst[i][:], sel, (dzer[:] if k == 0 else dstX), ALU.mult, ALU.add, ["d_cst%d" % i, "vecs", "XT%d%d" % (r, cc), "dzer"], ["XT%d%d" % (r, cc)])
            xkeys = ["XT%d%d" % (r, cc) for r in range(4) for cc in range(2)]
            fview = f_in.ap().rearrange("(q p) m -> q p m", p=128)
            ne = 0
            for mc in range(4):
                for b4 in range(16):
                    ps, pk = psum()
                    for bi in range(4):
                        bp = b4 * 4 + bi
                        for cc in range(2):
                            P.mm(ps[:, bi * 128:(bi + 1) * 128], XT[:, cc, bp::64], fcb[:, cc, mc, :], cc == 0, cc == 1, xkeys + ["fcb"], [pk])
                    ne += 1
                    P.copy(Zsb[:, b4 * 4:(b4 + 1) * 4, :], ps[:].rearrange("p (b m) -> p b m", m=128), [pk], ["Zsb"], eng="scalar" if ne % 2 else "vector")
                for m2 in range(32):
                    ps, pk = psum()
                    for mi in range(2):
                        m = m2 * 2 + mi
                        P.mm(ps[0:64, mi * 256:(mi + 1) * 256], Zsb[:, :, m], wab[:, 0, :], True, False, ["Zsb", "wab"], [pk])
                        P.mm(ps[0:64, mi * 256:(mi + 1) * 256], Zsb[:, :, 64 + m], wab[:, 1, :], False, True, ["Zsb", "wab"], [pk])
                    ne += 1
                    P.copy(Ysb[:, m2 * 2:(m2 + 1) * 2, :], ps[0:64, :].rearrange("p (m c) -> p m c", c=256), [pk], ["Ysb"], eng="scalar" if ne % 2 else "vector")
                for p8 in range(16):
                    ps, pk = psum()
                    for pi in range(8):
                        p = p8 * 8 + pi
                        P.mm(ps[0:64, pi * 64:(pi + 1) * 64], mpb[:, 0, p, :], Ysb[:, :, p], True, False, ["mpb", "Ysb"], [pk])
                        P.mm(ps[0:64, pi * 64:(pi + 1) * 64], mpb[:, 1, p, :], Ysb[:, :, 128 + p], False, True, ["mpb", "Ysb"], [pk])
                    ne += 1
                    P.copy(Fsb[:, p8 * 8:(p8 + 1) * 8, :], ps[0:64, :].rearrange("p (a m) -> p a m", m=64), [pk], ["Fsb"], eng="scalar" if ne % 2 else "vector")
                P.dma("sync", fview[:, :, mc * 64:(mc + 1) * 64], Fsb[:], reads=["Fsb"], writes=["f_in"])
            if debug:
                P.dma("sync", dbg["f"], f_in.ap(), reads=["f_in"])
        P.barrier()
        P.coll(f_in.ap(), f_all.ap(), ["f_in"], ["f_all"])
        fa = f_all.ap()

        with ExitStack() as sE:
            xT = sb("xT2", [128, 8, NT], F32, sE)
            for fc in range(8):
                P.dma("sync", xT[:, fc, :], x1_dr[fc], reads=["x1_dr"], writes=["xT"])
            with ExitStack() as sE1:
                FT = sb("FT", [128, 8, NT], BF16, sE1)
                est = [sb("e_st%d" % i, [128, 16, 256], BF16, sE1) for i in range(2)]
                facc = sb("e_acc", [128, 16, 256], F32, sE1)
                ezer = sb("ezer", [128, 16, 256], BF16, sE1)
                P.memset(ezer[:], 0.0, ["ezer"])
                nsel = 0
                for g in range(4):
                    for k in range(8):
                        bb_, jj = k // 4, k % 4
                        i = nsel % 2
                        nsel += 1
                        row0 = (4 * bb_ + g) * S + jj * NT
                        P.dma("sync", est[i][:], fa[row0:row0 + NT, :].rearrange("(t p) m -> p t m", p=128), reads=["f_all"], writes=["e_st%d" % i])
                        sel = vecs[:, V_SEL + k:V_SEL + k + 1]
                        P.stt(facc[:], est[i][:], sel, (ezer[:] if k == 0 else facc[:]), ALU.mult, ALU.add, ["e_st%d" % i, "vecs", "e_acc", "ezer"], ["e_acc"])
                    for tt_ in range(16):
                        ps, pk = psum()
                        for c2 in range(2):
                            P.op("tensor", lambda e, c2=c2, ps=ps, tt_=tt_: e.transpose(ps[:, c2 * 128:(c2 + 1) * 128], facc[:, tt_, c2 * 128:(c2 + 1) * 128], ident[:]),
                                 ["e_acc", "ident"], [pk])
                        P.copy(FT[:, g * 2:(g + 1) * 2, tt_ * 128:(tt_ + 1) * 128], ps[:, 0:256].rearrange("p (f t) -> p f t", t=128), [pk], ["FT"], eng="scalar")
                proj_res(1, wf_d, FT, "FT", xT, sE1)
            P.barrier()
            with ExitStack() as sE2:
                ffn(1, xT, sE2)
                ost = [sb("o_st%d" % i, [128, D], F32, sE2) for i in range(2)]
                for tt_ in range(NT // 128):
                    o = tt_ % 2
                    for g in range(2):
                        ps, pk = psum()
                        for f4 in range(4):
                            fc = g * 4 + f4
                            P.op("tensor", lambda e, fc=fc, f4=f4, ps=ps, tt_=tt_: e.transpose(ps[:, f4 * 128:(f4 + 1) * 128], xT[:, fc, tt_ * 128:(tt_ + 1) * 128], ident[:]),
                                 ["xT", "ident"], [pk])
                        P.copy(ost[o][:, g * 512:(g + 1) * 512], ps[:, :], [pk], ["o_st%d" % o], eng="scalar" if g else "vector")
                    P.dma("sync", out_d[tt_ * 128:(tt_ + 1) * 128, :], ost[o][:], reads=["o_st%d" % o], writes=["out"])

        P.wait("sync", [t for t in P.dlast if t is not None])
        P.emit()
    return nc


def _rope_tables(tok0):
    t = np.arange(tok0, tok0 + NT)
    row = (t // 64).astype(np.float32)
    col = (t % 64).astype(np.float32)
    half = 16
    inv = (1.0 / (10000.0 ** (np.arange(0, half, 2, dtype=np.float32) / half))).astype(np.float32)
    ar = row[:, None] * inv
    ac = col[:, None] * inv
    C = np.ones((96, NT), np.float32)
    Sn = np.zeros((96, NT), np.float32)
    for r in range(32):
        ang = ar if r < 16 else ac
        C[64 + r] = np.cos(ang[:, r % 8])
        Sn[64 + r] = np.sin(ang[:, r % 8])
    return C, Sn


def _consts():
    ident = np.eye(128, dtype=np.float32)
    pm = np.zeros((96, 96), np.float32)
    for r in range(32):
        d = 64 + r
        if (r % 16) < 8:
            pm[d, d + 8] = -1.0
        else:
            pm[d, d - 8] = 1.0
    pmT = np.ascontiguousarray(pm.T)
    shift = np.zeros((128, 128), np.float32)
    for k in range(128):
        shift[k, (k + 64) % 128] = 1.0
    c = np.arange(256)
    ang = 2 * np.pi * np.outer(c, c) / 256.0
    fcm = np.concatenate([np.cos(ang), -np.sin(ang)], 1).astype(np.float32)
    fcm = np.ascontiguousarray(fcm.reshape(2, 128, 2, 4, 64).transpose(1, 0, 3, 2, 4).reshape(128, 2, 4, 128))
    a = np.arange(128)
    ang = 2 * np.pi * np.outer(a, a) / 128.0
    Cm, Sm = np.cos(ang), np.sin(ang)
    wam = np.stack([np.concatenate([Cm, -Sm], 1), np.concatenate([Sm, Cm], 1)], 1).astype(np.float32)
    bq = np.arange(64)
    p = np.arange(128)
    ang = 2 * np.pi * (bq[:, None, None] * bq[None, None, :] / 64.0 + bq[:, None, None] * p[None, :, None] / 8192.0)
    sc = 1.0 / np.sqrt(8192.0 * 256.0)
    Mr = np.cos(ang) * sc
    Mi = -np.sin(ang) * sc
    mpm = np.stack([Mr, -Mi], 1).astype(np.float32)
    return dict(ident=ident, pmT=pmT, shiftm=shift, fcm=fcm, wam=wam, mpm=mpm)


def _prep_inputs(x, c, ctx, c_ctx, ada_w, ada_b, norm1_g, norm2_g, w_in, q_norm_g, kv_norm_g, w_uq, w_ukv,
                 q_gain, k_gain, conv_w, w_o, w_fourier, ffn_w1, ffn_w3, ffn_w2):
    f = lambda a: np.ascontiguousarray(np.asarray(a, dtype=np.float32))
    x, c, ctx, c_ctx = f(x), f(c), f(ctx), f(c_ctx)
    shared = _consts()
    shared["ada_w"] = f(np.asarray(ada_w).reshape(2, 8, 128, 12, 512).transpose(0, 3, 2, 1, 4).reshape(2, 12, 128, 8 * 512))
    shared["w_in"] = f(np.asarray(w_in)[0].reshape(8, 128, 2208).transpose(1, 0, 2))
    shared["w_uq"] = f(np.asarray(w_uq)[0].reshape(3, 128, 768).transpose(1, 0, 2))
    shared["w_ukv"] = f(np.asarray(w_ukv)[0].reshape(2, 128, 1024).transpose(1, 0, 2))
    colblk = lambda w, kc, nm: f(np.asarray(w).reshape(kc, 128, nm, 128).transpose(2, 1, 0, 3).reshape(nm, 128, kc * 128))
    shared["w_o"] = colblk(np.asarray(w_o)[0], 8, 8)
    shared["w_f"] = colblk(np.asarray(w_fourier)[0], 8, 8)
    shared["ffn_w1"] = np.stack([colblk(np.asarray(ffn_w1)[l], 8, NFF) for l in range(2)])
    shared["ffn_w3"] = np.stack([colblk(np.asarray(ffn_w3)[l], 8, NFF) for l in range(2)])
    shared["ffn_w2"] = np.stack([colblk(np.asarray(ffn_w2)[l], NFF, 8) for l in range(2)])
    in_maps = []
    for core in range(NCORES):
        b, j = core // 4, core % 4
        t0 = j * NT
        m = dict(shared)
        m["x"] = f(x[b, t0:t0 + NT])
        xh = np.zeros((2, D), np.float32)
        hm = np.zeros((2,), np.float32)
        if j > 0:
            xh[0] = x[b, t0 - 1]
            hm[0] = 1.0
        if j < 3:
            xh[1] = x[b, t0 + NT]
            hm[1] = 1.0
        m["xh"] = xh
        m["ctx"] = f(ctx[b])
        vecs = np.zeros((128, NVEC), np.float32)
        cs = np.stack([c[b], c_ctx], 0)
        vecs[:, V_CS:V_CS + 16] = cs.reshape(2, 8, 128).transpose(2, 1, 0).reshape(128, 16)
        for l in range(2):
            vecs[:, V_ADAB + 48 * l:V_ADAB + 48 * (l + 1)] = np.asarray(ada_b)[l].reshape(48, 128).T
            vecs[:, V_N1 + 8 * l:V_N1 + 8 * (l + 1)] = np.asarray(norm1_g)[l].reshape(8, 128).T
            vecs[:, V_N2 + 8 * l:V_N2 + 8 * (l + 1)] = np.asarray(norm2_g)[l].reshape(8, 128).T
        vecs[:, V_QNG:V_QNG + 3] = np.asarray(q_norm_g)[0].reshape(3, 128).T
        vecs[:, V_KVNG:V_KVNG + 2] = np.asarray(kv_norm_g)[0].reshape(2, 128).T
        vecs[0:96, V_QG] = np.asarray(q_gain)[0]
        vecs[0:96, V_KG] = np.asarray(k_gain)[0]
        vecs[:, V_CONV:V_CONV + 12] = np.asarray(conv_w)[0].reshape(3, 4, 128).transpose(2, 0, 1).reshape(128, 12)
        vecs[:, V_HM:V_HM + 2] = hm[None, :]
        vecs[:, V_MB + b] = 1.0
        vecs[:, V_SEL + core] = 1.0
        m["vecs"] = vecs
        C, Sn = _rope_tables(t0)
        m["ropec"], m["ropes"] = C, Sn
        in_maps.append(m)
    return in_maps


def kernel(**inputs):
    in_maps = _prep_inputs(**inputs)
    nc = build()
    res = run_bass_kernel_spmd(nc, in_maps, core_ids=list(range(NCORES)))
    out = np.zeros((NB, S, D), np.float32)
    for core in range(NCORES):
        b, j = core // 4, core % 4
        out[b, j * NT:(j + 1) * NT] = res.results[core]["out"]
    return out
```

```python
import numpy as np
from contextlib import ExitStack
import concourse.bass as bass
import concourse.mybir as mybir
from concourse.bass_utils import run_bass_kernel_spmd

F32 = mybir.dt.float32
BF16 = mybir.dt.bfloat16
AF = mybir.ActivationFunctionType
ALU = mybir.AluOpType

D = 1024
S = 8192
NB = 2
LCTX = 256
NT = 2048
H = 8
DFF = 2816
NFF = 22
EPS = 1e-6
QK_SCALE = 96 ** -0.5
NCORES = 8

V_CS = 0
V_ADAB = 16
V_N1 = 112
V_N2 = 128
V_QNG = 144
V_KVNG = 147
V_QG = 149
V_KG = 150
V_CONV = 151
V_HM = 163
V_MB = 165
V_SEL = 167
NVEC = 175


class Sem:
    def __init__(self, h):
        self.h = h
        self.count = 0


class Prog:
    ENG = ["sync", "scalar", "vector", "gpsimd", "tensor"]

    def __init__(self, nc, es, ndma=24):
        self.nc = nc
        self.q = {e: [] for e in self.ENG}
        self.esem = {e: Sem(es.enter_context(nc.semaphore("s_" + e))) for e in ["scalar", "vector", "gpsimd", "tensor"]}
        self.dsems = [Sem(es.enter_context(nc.semaphore("d%d" % i))) for i in range(ndma)]
        self.dlast = [None] * ndma
        self.dnext = 0
        self.waited = {e: {} for e in self.ENG}
        self.lastw = {}
        self.reads = {}
        self.nps = 0
        self.groups = {}

    def _x(self, keys):
        out = []
        for k in keys:
            out.extend(self.groups.get(k, []))
            out.append(k)
        return out

    def _deps(self, reads, writes):
        reads, writes = self._x(reads), self._x(writes)
        toks = []
        for k in reads:
            if k in self.lastw:
                toks.append(self.lastw[k])
        for k in writes:
            if k in self.lastw:
                toks.append(self.lastw[k])
            toks.extend(self.reads.get(k, []))
        return toks

    def _commit(self, tok, reads, writes):
        reads, writes = self._x(reads), self._x(writes)
        for k in writes:
            self.lastw[k] = tok
            self.reads[k] = []
        for k in reads:
            if k not in writes:
                self.reads.setdefault(k, []).append(tok)

    def _waits(self, eng, toks):
        out = []
        for t in toks:
            if t is None:
                continue
            sem, val, src = t
            if src == eng and eng == "tensor":
                continue
            w = self.waited[eng]
            if w.get(id(sem), 0) >= val:
                continue
            w[id(sem)] = val
            out.append((sem, val))
        return out

    def op(self, eng, fn, reads=(), writes=(), extra=()):
        toks = self._deps(reads, writes) + list(extra)
        waits = self._waits(eng, toks)
        sem = self.esem[eng]
        sem.count += 1
        tok = (sem, sem.count, eng)
        self.q[eng].append((waits, fn, sem, 1))
        self._commit(tok, reads, writes)
        return tok

    def dma(self, eng, out, in_, reads=(), writes=(), extra=(), **kw):
        i = self.dnext % len(self.dsems)
        self.dnext += 1
        sem = self.dsems[i]
        toks = self._deps(reads, writes) + list(extra) + [self.dlast[i]]
        waits = self._waits(eng, toks)
        sem.count += 16
        tok = (sem, sem.count, "dma")
        self.dlast[i] = tok
        self.q[eng].append((waits, lambda e: e.dma_start(out=out, in_=in_, **kw), sem, 16))
        self._commit(tok, reads, writes)
        return tok

    def coll(self, ins_ap, outs_ap, reads, writes):
        if not hasattr(self, "cc"):
            raise RuntimeError("no cc sem")
        toks = self._deps(reads, writes) + [self.cc_last]
        waits = self._waits("gpsimd", toks)
        self.cc.count += 1
        tok = (self.cc, self.cc.count, "cc")
        self.cc_last = tok
        self.q["gpsimd"].append((waits, lambda e: e.collective_compute("AllGather", ALU.bypass, replica_groups=[list(range(NCORES))], ins=[ins_ap], outs=[outs_ap]), self.cc, 1))
        self._commit(tok, reads, writes)
        return tok

    def barrier(self):
        toks = [(sm, sm.count, name) for name, sm in self.esem.items() if sm.count > 0]
        toks += [t for t in self.dlast if t is not None]
        if self.cc_last is not None:
            toks.append(self.cc_last)
        for e in self.ENG:
            self.wait(e, toks)

    def wait(self, eng, toks):
        waits = self._waits(eng, toks)
        if waits:
            self.q[eng].append((waits, None, None, 0))

    def emit(self):
        nc = self.nc
        with nc.Block() as block:
            def mk(name):
                def body(e):
                    for waits, fn, sem, amt in self.q[name]:
                        for s, v in waits:
                            e.wait_ge(s.h, v)
                        if fn is not None:
                            ins = fn(e)
                            if sem is not None:
                                ins.then_inc(sem.h, amt)
                return body
            block.sync(mk("sync"))
            block.scalar(mk("scalar"))
            block.vector(mk("vector"))
            block.gpsimd(mk("gpsimd"))
            block.tensor(mk("tensor"))

    def mm(self, out, lhsT, rhs, start, stop, reads, writes):
        return self.op("tensor", lambda e: e.matmul(out, lhsT, rhs, start=start, stop=stop), reads, writes)

    def act(self, out, in_, func, reads, writes, eng="scalar", **kw):
        return self.op(eng, lambda e: e.activation(out=out, in_=in_, func=func, **kw), reads, writes)

    def tt(self, out, in0, in1, op, reads, writes, eng="vector"):
        return self.op(eng, lambda e: e.tensor_tensor(out=out, in0=in0, in1=in1, op=op), reads, writes)

    def stt(self, out, in0, scalar, in1, op0, op1, reads, writes, eng="vector"):
        if eng == "gpsimd":
            self.ts(in0, in0, scalar, None, op0, None, reads, [reads[0]], eng=eng)
            return self.tt(out, in0, in1, op1, reads, writes, eng=eng)
        return self.op(eng, lambda e: e.scalar_tensor_tensor(out=out, in0=in0, scalar=scalar, in1=in1, op0=op0, op1=op1), reads, writes)

    def ts(self, out, in0, s1, s2, op0, op1, reads, writes, eng="vector"):
        if s2 is None:
            return self.op(eng, lambda e: e.tensor_scalar(out=out, in0=in0, scalar1=s1, scalar2=None, op0=op0), reads, writes)
        return self.op(eng, lambda e: e.tensor_scalar(out=out, in0=in0, scalar1=s1, scalar2=s2, op0=op0, op1=op1), reads, writes)

    def copy(self, out, in_, reads, writes, eng="vector"):
        if eng == "scalar":
            return self.op(eng, lambda e: e.activation(out=out, in_=in_, func=AF.Copy), reads, writes)
        return self.op(eng, lambda e: e.tensor_copy(out=out, in_=in_), reads, writes)

    def memset(self, ap, val, writes, eng="vector"):
        return self.op(eng, lambda e: e.memset(ap, val), (), writes)

    def recip(self, out, in_, reads, writes):
        return self.op("vector", lambda e: e.reciprocal(out=out, in_=in_), reads, writes)


def build(debug=False, stop_after=None):
    nc = bass.Bass("TRN2", target_bir_lowering=False)

    def din(name, shape, dt=F32):
        return nc.dram_tensor(name, list(shape), dt, kind="ExternalInput").ap()

    def dout(name, shape, dt=F32):
        return nc.dram_tensor(name, list(shape), dt, kind="ExternalOutput").ap()

    x_d = din("x", [NT, D])
    xh_d = din("xh", [2, D])
    ctx_d = din("ctx", [LCTX, D])
    vecs_d = din("vecs", [128, NVEC])
    adaw_d = din("ada_w", [2, 12, 128, 8 * 512])
    win_d = din("w_in", [128, 8, 2208])
    wuq_d = din("w_uq", [128, 3, 768])
    wukv_d = din("w_ukv", [128, 2, 1024])
    wo_d = din("w_o", [8, 128, 8 * 128])
    wf_d = din("w_f", [8, 128, 8 * 128])
    w1_d = din("ffn_w1", [2, NFF, 128, 8 * 128])
    w3_d = din("ffn_w3", [2, NFF, 128, 8 * 128])
    w2_d = din("ffn_w2", [2, 8, 128, NFF * 128])
    ident_d = din("ident", [128, 128])
    ropec_d = din("ropec", [96, NT])
    ropes_d = din("ropes", [96, NT])
    pmT_d = din("pmT", [96, 96])
    shift_d = din("shiftm", [128, 128])
    fc_d = din("fcm", [128, 2, 4, 128])
    wa_d = din("wam", [128, 2, 256])
    mp_d = din("mpm", [64, 2, 128, 64])
    out_d = dout("out", [NT, D])

    q_dr = nc.dram_tensor("q_dr", [H, 96, NT], BF16).ap()
    s_dr = nc.dram_tensor("s_dr", [4, 128, NT + 1], BF16).ap()
    kc_dr = nc.dram_tensor("kc_dr", [H, 96, LCTX], BF16).ap()
    vc_dr = nc.dram_tensor("vc_dr", [LCTX, 512], BF16).ap()
    kv_in = nc.dram_tensor("kv_in", [1280, NT], BF16)
    kv_all = nc.dram_tensor("kv_all", [NCORES * 1280, NT], BF16)
    h1_in = nc.dram_tensor("h1_in", [D, NT], BF16)
    h1_all = nc.dram_tensor("h1_all", [NCORES * D, NT], BF16)
    f_in = nc.dram_tensor("f_in", [S, 256], BF16)
    f_all = nc.dram_tensor("f_all", [NCORES * S, 256], BF16)
    x1_dr = nc.dram_tensor("x1_dr", [8, 128, NT], F32).ap()

    dbg = {}
    if debug:
        dbg["q"] = dout("dbg_q", [H, 96, NT], BF16)
        dbg["kv"] = dout("dbg_kv", [1280, NT], BF16)
        dbg["s"] = dout("dbg_s", [4, 128, NT + 1], BF16)
        dbg["kc"] = dout("dbg_kc", [H, 96, LCTX], BF16)
        dbg["vc"] = dout("dbg_vc", [LCTX, 512], BF16)
        dbg["mod"] = dout("dbg_mod", [128, 2 * 96])
        dbg["x1"] = dout("dbg_x1", [8, 128, NT])
        dbg["h1"] = dout("dbg_h1", [D, NT], BF16)
        dbg["f"] = dout("dbg_f", [S, 256], BF16)
        dbg["a"] = dout("dbg_a", [4, 128, NT], BF16)

    with ExitStack() as es:
        P = Prog(nc, es)
        P.cc = Sem(es.enter_context(nc.semaphore("cc")))
        P.cc_last = None

        def sb(name, shape, dt=F32, stack=es):
            return stack.enter_context(nc.sbuf_tensor("t_" + name, list(shape), dt))

        psb = [es.enter_context(nc.psum_tensor("ps%d" % i, [128, 512], F32)) for i in range(8)]

        def psum():
            i = P.nps % 8
            P.nps += 1
            return psb[i], "ps%d" % i

        wst = [sb("wst%d" % i, [128, 4096]) for i in range(2)]
        nst = [0]

        def load_cast(dst, src_d, ncols, dkey):
            i = nst[0] % 2
            nst[0] += 1
            Pn = dst.shape[0]
            P.dma("sync", wst[i][0:Pn, 0:ncols], src_d, writes=["wst%d" % i])
            if ncols < 512:
                P.copy(dst, wst[i][0:Pn, 0:ncols], ["wst%d" % i], [dkey], eng="vector")
            else:
                P.groups[dkey] = [dkey + "_a", dkey + "_b", dkey + "_c"]
                c1 = (ncols * 14 // 100) // 8 * 8
                c2 = c1 + (ncols * 46 // 100) // 8 * 8
                P.copy(dst[:, 0:c1], wst[i][0:Pn, 0:c1], ["wst%d" % i], [dkey + "_a"], eng="gpsimd")
                P.copy(dst[:, c1:c2], wst[i][0:Pn, c1:c2], ["wst%d" % i], [dkey + "_b"], eng="vector")
                P.copy(dst[:, c2:ncols], wst[i][0:Pn, c2:ncols], ["wst%d" % i], [dkey + "_c"], eng="scalar")

        vecs = sb("vecs", [128, NVEC])
        ident = sb("ident", [128, 128])
        identb = sb("identb", [128, 128], BF16)
        onesb = sb("onesb", [128, 128], BF16)
        mod = sb("mod", [128, 2, 48, 2])
        A1 = sb("A1", [128, 2, 8, 2])
        A2 = sb("A2", [128, 2, 8, 2])
        gsc = sb("gsc", [128, 2])
        P.dma("sync", vecs[:], vecs_d, writes=["vecs"])
        P.dma("sync", ident[:], ident_d, writes=["ident"])
        P.copy(identb[:], ident[:], ["ident"], ["identb"])
        P.memset(onesb[:], 1.0, ["onesb"])

        def pnorm(chunks, keys, nfeat, N, tmp_pool, sfx=""):
            Pi = chunks[0].shape[0]
            sq, rs = tmp_pool
            ps, pk = psum()
            for i, (c, k) in enumerate(zip(chunks, keys)):
                sqi = sq[0:Pi, i % 2, 0:N]
                P.act(sqi, c, AF.Square, [k], ["sq%d" % (i % 2) + sfx])
                P.mm(ps[0:Pi, 0:N], onesb[0:Pi, 0:Pi], sqi, i == 0, i == len(chunks) - 1, ["sq%d" % (i % 2) + sfx, "onesb"], [pk])
            P.act(rs[0:Pi, 0:N], ps[0:Pi, 0:N], AF.Sqrt, [pk], ["rs" + sfx], scale=1.0 / nfeat, bias=EPS)
            P.recip(rs[0:Pi, 0:N], rs[0:Pi, 0:N], ["rs" + sfx], ["rs" + sfx])
            return rs[0:Pi, 0:N], "rs" + sfx

        with ExitStack() as s0:
            csb = sb("csb", [128, 16], BF16, s0)
            wts = [sb("adawt%d" % i, [128, 8, 512], BF16, s0) for i in range(2)]
            P.act(csb[:], vecs[:, V_CS:V_CS + 16], AF.Silu, ["vecs"], ["csb"])
            n = 0
            for l in range(2):
                ps, pk = psum()
                for nb in range(12):
                    wt = wts[n % 2]
                    wk_ = "adawt%d" % (n % 2)
                    n += 1
                    load_cast(wt[:].rearrange("p a b -> p (a b)"), adaw_d[l, nb], 4096, wk_)
                    for jj in range(4):
                        j = nb * 4 + jj
                        for kc in range(8):
                            P.mm(ps[:, 2 * j:2 * j + 2], wt[:, kc, jj * 128:(jj + 1) * 128], csb[:, 2 * kc:2 * kc + 2],
                                 kc == 0, kc == 7, [wk_, "csb"], [pk])
                P.tt(mod[:, l], ps[:, 0:96].rearrange("p (a b) -> p a b", b=2),
                     vecs[:, V_ADAB + 48 * l:V_ADAB + 48 * (l + 1)].unsqueeze(2).to_broadcast([128, 48, 2]),
                     ALU.add, [pk, "vecs"], ["mod"])
                P.stt(A1[:, l], mod[:, l, 8:16, :], 1.0, vecs[:, V_N1 + 8 * l:V_N1 + 8 * (l + 1)].unsqueeze(2).to_broadcast([128, 8, 2]),
                      ALU.add, ALU.mult, ["mod", "vecs"], ["A1"])
                P.stt(A2[:, l], mod[:, l, 32:40, :], 1.0, vecs[:, V_N2 + 8 * l:V_N2 + 8 * (l + 1)].unsqueeze(2).to_broadcast([128, 8, 2]),
                      ALU.add, ALU.mult, ["mod", "vecs"], ["A2"])
            P.act(gsc[:, 0:1], vecs[:, V_QG:V_QG + 1], AF.Copy, ["vecs"], ["gsc"], scale=QK_SCALE)
            P.act(gsc[:, 1:2], vecs[:, V_KG:V_KG + 1], AF.Copy, ["vecs"], ["gsc"])
            if debug:
                P.dma("sync", dbg["mod"], mod[:].rearrange("p l a b -> p (l a b)"), reads=["mod"])
        P.barrier()
        def mv(l, idx, fc, j):
            return mod[:, l, idx * 8 + fc, j:j + 1]

        with ExitStack() as sa:
            winb = sb("winb", [128, 8, 2208], BF16, sa)
            wpe = sb("wpe", [128, 8, 96], BF16, sa)
            wuqb = sb("wuqb", [128, 3, 768], BF16, sa)
            wukvb = sb("wukvb", [128, 2, 1024], BF16, sa)
            wk = sb("wk", [128, 2, 8, 96], BF16, sa)
            wv = sb("wv", [128, 2, 8, 64], BF16, sa)
            ropec = sb("ropec", [96, NT], F32, sa)
            ropes = sb("ropes", [96, NT], F32, sa)
            pmT = sb("pmT", [96, 96], BF16, sa)
            xTb = sb("xTb", [128, 8, 512], F32, sa)
            stage = sb("stage", [128, D], F32, sa)
            hT = sb("hT", [128, 8, 512], BF16, sa)
            sq = sb("sq", [128, 2, 512], BF16, sa)
            rs = sb("rs", [128, 512], F32, sa)
            tmpf = sb("tmpf", [128, 512], F32, sa)
            cqf = sb("cqf", [128, 5, 512], F32, sa)
            cqn = sb("cqn", [128, 5, 512], BF16, sa)
            qns = [sb("qn%d" % i, [96, 512], BF16, sa) for i in range(2)]
            t1s = [sb("t1_%d" % i, [96, 512], F32, sa) for i in range(2)]
            t2s = [sb("t2_%d" % i, [96, 512], F32, sa) for i in range(2)]
            sqs = [sq, sb("sq_b", [128, 2, 512], BF16, sa)]
            rss = [rs, sb("rs_b", [128, 512], F32, sa)]
            qo = sb("qo", [96, 2, 512], BF16, sa)
            vst = sb("vst", [128, 2, 512], BF16, sa)
            zb = sb("zb", [128, 4, 514], F32, sa)
            bb = sb("bb", [128, 4, 513], F32, sa)
            uf = sb("uf", [128, 512], F32, sa)
            yt = sb("yt", [128, 512], F32, sa)
            so = sb("so", [128, 2, 512], BF16, sa)

            for kc in range(8):
                load_cast(winb[:, kc, :], win_d[:, kc, :], 2208, "winb")
            load_cast(wuqb[:].rearrange("p a b -> p (a b)"), wuq_d.rearrange("p a b -> p (a b)"), 3 * 768, "wuqb")
            load_cast(wukvb[:].rearrange("p a b -> p (a b)"), wukv_d.rearrange("p a b -> p (a b)"), 2048, "wukvb")
            load_cast(pmT[:], pmT_d, 96, "pmT")
            P.dma("sync", ropec[:], ropec_d, writes=["ropec"])
            P.dma("sync", ropes[:], ropes_d, writes=["ropes"])
            P.memset(wpe[:], 0.0, ["wpe"])
            P.memset(wk[:], 0.0, ["wk"])
            P.memset(zb[:], 0.0, ["zb"])
            P.memset(bb[:], 0.0, ["bb"])
            P.copy(wpe[:, :, 64:96], winb[:, :, 640:672], ["winb"], ["wpe"])
            wukv4 = wukvb[:].rearrange("p k (h d) -> p k h d", d=128)
            P.copy(wk[:, :, :, 0:64], wukv4[:, :, :, 0:64], ["wukvb"], ["wk"])
            P.copy(wv[:], wukv4[:, :, :, 64:128], ["wukvb"], ["wv"])

            nq = [0]

            def load_T(rows_ap, ntok, col0):
                P.dma("sync", stage[0:ntok, :], rows_ap, writes=["stage"])
                for g in range(2):
                    ps, pk = psum()
                    for f4 in range(4):
                        fc = g * 4 + f4
                        P.op("tensor", lambda e, fc=fc, f4=f4, ps=ps: e.transpose(ps[:, f4 * 128:f4 * 128 + ntok], stage[0:ntok, fc * 128:(fc + 1) * 128], ident[0:ntok, 0:ntok]),
                             ["stage", "ident"], [pk])
                    P.copy(xTb[:, g * 4:(g + 1) * 4, col0:col0 + ntok],
                           ps[:].rearrange("p (f t) -> p f t", t=128)[:, :, 0:ntok], [pk], ["xTb"], eng="scalar")

            def rms_mod(src, srck, N, Asc, Bsh, dst, dstk):
                rstd, rk = pnorm([src[:, fc, 0:N] for fc in range(8)], [srck] * 8, D, N, (sq, rs))
                for fc in range(8):
                    P.stt(tmpf[:, 0:N], src[:, fc, 0:N], Asc(fc), rstd, ALU.mult, ALU.mult, [srck, rk, "A1", "A2"], ["tmpf"])
                    P.act(dst[:, fc, 0:N], tmpf[:, 0:N], AF.Identity, ["tmpf", "mod"], [dstk], bias=Bsh(fc))

            def lin_in(col0, ncols, N, wsrc=None):
                ps, pk = psum()
                for kc in range(8):
                    P.mm(ps[0:ncols, 0:N], winb[:, kc, col0:col0 + ncols], hT[:, kc, 0:N], kc == 0, kc == 7, ["winb", "hT"], [pk])
                return ps, pk

            def lora_norm(col0, nch, gcol, N, base):
                for i in range(nch):
                    ps, pk = lin_in(col0 + 128 * i, 128, N)
                    P.copy(cqf[:, base + i, 0:N], ps[:, 0:N], [pk], ["cqf%d" % base], eng="scalar")
                rstd, rk = pnorm([cqf[:, base + i, 0:N] for i in range(nch)], ["cqf%d" % base] * nch, nch * 128, N, (sq, rs))
                for i in range(nch):
                    P.stt(cqn[:, base + i, 0:N], cqf[:, base + i, 0:N], vecs[:, gcol + i:gcol + i + 1], rstd, ALU.mult, ALU.mult,
                          ["cqf%d" % base, rk, "vecs"], ["cqn%d" % base])

            def head_norm_rope(ps, pk, N, gcol, tok0, rope, out_ap, outk, st):
                t1_, t2_, qn_ = t1s[st], t2s[st], qns[st]
                x = "_%d" % st
                P.copy(t1_[:, 0:N], ps[0:96, 0:N], [pk], ["t1" + x], eng="scalar")
                rstd, rk = pnorm([t1_[:, 0:N]], ["t1" + x], 96, N, (sqs[st], rss[st]), sfx=x)
                if not rope:
                    P.stt(out_ap, t1_[:, 0:N], gsc[0:96, gcol:gcol + 1], rstd, ALU.mult, ALU.mult, ["t1" + x, rk, "gsc"], [outk])
                    return
                P.stt(qn_[:, 0:N], t1_[:, 0:N], gsc[0:96, gcol:gcol + 1], rstd, ALU.mult, ALU.mult, ["t1" + x, rk, "gsc"], ["qn" + x])
                ps2, pk2 = psum()
                P.mm(ps2[0:96, 0:N], pmT[:], qn_[:, 0:N], True, True, ["pmT", "qn" + x], [pk2])
                P.tt(t2_[:, 0:N], ps2[0:96, 0:N], ropes[:, tok0:tok0 + N], ALU.mult, [pk2, "ropes"], ["t2" + x])
                P.tt(t1_[:, 0:N], qn_[:, 0:N], ropec[:, tok0:tok0 + N], ALU.mult, ["qn" + x, "ropec"], ["t1" + x])
                P.tt(out_ap, t1_[:, 0:N], t2_[:, 0:N], ALU.add, ["t1" + x, "t2" + x], [outk])

            def proc(kind, N, tok0):
                j = 1 if kind == "ctx" else 0
                rms_mod(xTb, "xTb", N, lambda fc: A1[:, 0, fc, j:j + 1], lambda fc: mv(0, 0, fc, j), hT, "hT")
                if kind == "lat":
                    lora_norm(0, 3, V_QNG, N, 0)
                    for h in range(H):
                        ps, pk = psum()
                        for kc in range(3):
                            P.mm(ps[0:96, 0:N], wuqb[:, kc, h * 96:(h + 1) * 96], cqn[:, kc, 0:N], kc == 0, kc == 2, ["wuqb", "cqn0"], [pk])
                        o = nq[0] % 2
                        nq[0] += 1
                        head_norm_rope(ps, pk, N, 0, tok0, True, qo[:, o, 0:N], "qo%d" % o, o)
                        P.dma("sync", q_dr[h, :, tok0:tok0 + N], qo[:, o, 0:N], reads=["qo%d" % o], writes=["q_dr"])
                if kind in ("lat", "ctx"):
                    lora_norm(384, 2, V_KVNG, N, 3)
                    for h in range(H):
                        ps, pk = psum()
                        for kc in range(2):
                            P.mm(ps[0:96, 0:N], wk[:, kc, h, :], cqn[:, 3 + kc, 0:N], kc == 0, False, ["wk", "cqn3"], [pk])
                        for kc in range(8):
                            P.mm(ps[0:96, 0:N], wpe[:, kc, :], hT[:, kc, 0:N], False, kc == 7, ["wpe", "hT"], [pk])
                        o = nq[0] % 2
                        nq[0] += 1
                        head_norm_rope(ps, pk, N, 1, tok0, kind == "lat", qo[:, o, 0:N], "qo%d" % o, o)
                        if kind == "lat":
                            P.dma("sync", kv_in.ap()[h * 96:(h + 1) * 96, tok0:tok0 + N], qo[:, o, 0:N], reads=["qo%d" % o], writes=["kv_in"])
                        else:
                            P.dma("sync", kc_dr[h, :, :], qo[:, o, 0:N], reads=["qo%d" % o], writes=["kc_dr"])
                    for tt_ in range(N // 128):
                        ps, pk = psum()
                        for kc in range(2):
                            P.mm(ps[:, :], cqn[:, 3 + kc, tt_ * 128:(tt_ + 1) * 128], wv[:, kc].rearrange("p h d -> p (h d)"),
                                 kc == 0, kc == 1, ["wv", "cqn3"], [pk])
                        o = nq[0] % 2
                        nq[0] += 1
                        P.copy(vst[:, o, :], ps[:, :], [pk], ["vst%d" % o], eng="scalar")
                        if kind == "lat":
                            t0 = tok0 + tt_ * 128
                            dst = kv_in.ap()[768 + t0 // 4:768 + t0 // 4 + 32, :].rearrange("r (q c) -> (r q) c", c=512)
                            P.dma("sync", dst, vst[:, o, :], reads=["vst%d" % o], writes=["kv_in"])
                        else:
                            P.dma("sync", vc_dr[tt_ * 128:(tt_ + 1) * 128, :], vst[:, o, :], reads=["vst%d" % o], writes=["vc_dr"])
                if kind == "lat":
                    for ch in range(4):
                        psb_, pkb = lin_in(672 + 128 * ch, 128, N)
                        P.copy(bb[:, ch, 1:1 + N], psb_[:, 0:N], [pkb], ["bb"], eng="scalar")
                        psu, pku = lin_in(1696 + 128 * ch, 128, N)
                        P.copy(uf[:, 0:N], psu[:, 0:N], [pku], ["uf"], eng="scalar")
                        psc, pkc = lin_in(1184 + 128 * ch, 128, N)
                        P.tt(zb[:, ch, 2:2 + N], psc[:, 0:N], uf[:, 0:N], ALU.mult, [pkc, "uf"], ["zb"])
                    conv_out(N, tok0)
                if kind == "halo":
                    for ch in range(4):
                        psu, pku = lin_in(1696 + 128 * ch, 128, N)
                        P.copy(uf[:, 0:N], psu[:, 0:N], [pku], ["uf"], eng="scalar")
                        psc, pkc = lin_in(1184 + 128 * ch, 128, N)
                        P.tt(uf[:, 0:N], psc[:, 0:N], uf[:, 0:N], ALU.mult, [pkc, "uf"], ["uf"])
                        P.tt(zh[:, ch, :], uf[:, 0:2], vecs[:, V_HM:V_HM + 2], ALU.mult, ["uf", "vecs"], ["zh"])

            def conv_out(N, tok0):
                for ch in range(4):
                    P.ts(yt[:, 0:N], zb[:, ch, 0:N], vecs[:, V_CONV + ch:V_CONV + ch + 1], None, ALU.mult, ALU.bypass, ["zb", "vecs"], ["yt"])
                    P.stt(yt[:, 0:N], zb[:, ch, 1:1 + N], vecs[:, V_CONV + 4 + ch:V_CONV + 5 + ch], yt[:, 0:N], ALU.mult, ALU.add, ["zb", "vecs", "yt"], ["yt"])
                    P.stt(yt[:, 0:N], zb[:, ch, 2:2 + N], vecs[:, V_CONV + 8 + ch:V_CONV + 9 + ch], yt[:, 0:N], ALU.mult, ALU.add, ["zb", "vecs", "yt"], ["yt"])
                    o = nq[0] % 2
                    nq[0] += 1
                    P.tt(so[:, o, 0:N], yt[:, 0:N], bb[:, ch, 0:N], ALU.mult, ["yt", "bb"], ["so%d" % o])
                    P.dma("sync", s_dr[ch, :, tok0:tok0 + N], so[:, o, 0:N], reads=["so%d" % o], writes=["s_dr"], allow_slow_non_contiguous=True)
                P.copy(zb[:, :, 0:2], zb[:, :, N:N + 2], ["zb"], ["zb"])
                P.copy(bb[:, :, 0:1], bb[:, :, N:N + 1], ["bb"], ["bb"])

            zh = sb("zh", [128, 4, 2], F32, sa)
            load_T(xh_d, 2, 0)
            proc("halo", 2, 0)
            P.copy(zb[:, :, 1:2], zh[:, :, 0:1], ["zh", "zb"], ["zb"])
            for tt_ in range(2):
                load_T(ctx_d[tt_ * 128:(tt_ + 1) * 128, :], 128, tt_ * 128)
            proc("ctx", 256, 0)
            for blk in range(4):
                for tt_ in range(4):
                    load_T(x_d[blk * 512 + tt_ * 128:blk * 512 + (tt_ + 1) * 128, :], 128, tt_ * 128)
                proc("lat", 512, blk * 512)
            P.copy(zb[:, :, 2:3], zh[:, :, 1:2], ["zh", "zb"], ["zb"])
            conv_out(1, NT)

            if debug:
                P.dma("sync", dbg["q"], q_dr, reads=["q_dr"])
                P.dma("sync", dbg["kv"], kv_in.ap(), reads=["kv_in"])
                P.dma("sync", dbg["s"], s_dr, reads=["s_dr"])
                P.dma("sync", dbg["kc"], kc_dr, reads=["kc_dr"])
                P.dma("sync", dbg["vc"], vc_dr, reads=["vc_dr"])


        P.barrier()

        def psr(lo, hi, ctr):
            i = lo + ctr[0] % (hi - lo)
            ctr[0] += 1
            return psb[i], "ps%d" % i

        a_dr = nc.dram_tensor("a_dr", [4, 128, NT], BF16).ap()
        P.coll(kv_in.ap(), kv_all.ap(), ["kv_in"], ["kv_all"])
        kva = kv_all.ap()

        with ExitStack() as sB:
            kT = [sb("kT%d" % i, [96, S + LCTX], BF16, sB) for i in range(2)]
            vaug = [sb("vaug%d" % i, [128, 66, 128], BF16, sB) for i in range(2)]
            kst = [sb("kst%d" % i, [96, 2, NT], BF16, sB) for i in range(2)]
            vsg = [sb("vsg%d" % i, [128, 2, 16, 64], BF16, sB) for i in range(2)]
            qh = [sb("qh%d" % i, [96, NT], BF16, sB) for i in range(2)]
            pT = [sb("pT%d" % i, [128, 512], BF16, sB) for i in range(4)]
            OS = [sb("OS%d" % i, [128, 512], F32, sB) for i in range(2)]
            Rr = sb("Rr", [128, 512], F32, sB)
            ao = [sb("ao%d" % i, [128, 512], BF16, sB) for i in range(2)]
            shiftm = sb("shiftm", [128, 128], F32, sB)
            P.dma("sync", shiftm[:], shift_d, writes=["shiftm"])
            for i in range(2):
                P.memset(vaug[i][:], 1.0, ["vaug%d" % i], eng="gpsimd")
            m0 = vecs[:, V_MB:V_MB + 1]
            m1 = vecs[:, V_MB + 1:V_MB + 2]
            nsel = [0]
            sctr = [0]
            octr = [0]
            zer = sb("zer", [128, NT], BF16, sB)
            P.memset(zer[:], 0.0, ["zer"])

            def load_head(h):
                hb = h % 2
                off = 0 if hb == 0 else 64
                kk, vk, qk = "kT%d" % hb, "vaug%d" % hb, "qh%d" % hb
                P.dma("sync", qh[hb][:], q_dr[h], reads=["q_dr"], writes=[qk])
                for r in range(4):
                    i = nsel[0] % 2
                    nsel[0] += 1
                    for cb in range(2):
                        R = 4 * cb + r
                        P.dma("sync", kst[i][:, cb, :], kva[R * 1280 + h * 96:R * 1280 + (h + 1) * 96, :], reads=["kv_all"], writes=["kst%d_%d" % (i, cb)])
                        vsrc = kva[R * 1280 + 768:R * 1280 + 1280, :].rearrange("r (q c) -> (r q) c", c=512).rearrange("(t p) c -> p t c", p=128)[:, :, h * 64:(h + 1) * 64]
                        P.dma("sync", vsg[i][:, cb], vsrc, reads=["kv_all"], writes=["vsg%d_%d" % (i, cb)])
                    kdst = kT[hb][:, r * NT:(r + 1) * NT]
                    P.stt(kdst, kst[i][:, 0, :], m0[0:96], zer[0:96, :], ALU.mult, ALU.add, ["kst%d_0" % i, "vecs", "zer"], [kk])
                    P.stt(kdst, kst[i][:, 1, :], m1[0:96], kdst, ALU.mult, ALU.add, ["kst%d_1" % i, "vecs", kk], [kk])
                    vdst = vaug[hb][:, r * 16:(r + 1) * 16, off:off + 64]
                    zv = zer[:, 0:1024].rearrange("p (t c) -> p t c", c=64)
                    P.stt(vdst, vsg[i][:, 0], m0, zv, ALU.mult, ALU.add, ["vsg%d_0" % i, "vecs", "zer"], [vk])
                    P.stt(vdst, vsg[i][:, 1], m1, vdst, ALU.mult, ALU.add, ["vsg%d_1" % i, "vecs", vk], [vk])
                P.dma("sync", kT[hb][:, S:S + LCTX], kc_dr[h], reads=["kc_dr"], writes=[kk])
                P.dma("sync", vaug[hb][:, 64:66, off:off + 64], vc_dr.rearrange("(t p) c -> p t c", p=128)[:, :, h * 64:(h + 1) * 64], reads=["vc_dr"], writes=[vk])

            def compute_head(h):
                hb = h % 2
                kk, vk, qk = "kT%d" % hb, "vaug%d" % hb, "qh%d" % hb
                for qb in range(4):
                    po, pok = psr(6, 8, octr)
                    NK = 66
                    sb_list = []

                    def s_mm(kt):
                        ps_, psk = psr(0, 6, sctr)
                        P.mm(ps_[:, :], kT[hb][:, kt * 128:(kt + 1) * 128], qh[hb][:, qb * 512:(qb + 1) * 512], True, True, [kk, qk], [psk])
                        sb_list.append((ps_, psk))
                    s_mm(0)
                    s_mm(1)
                    for kt in range(NK):
                        if kt + 2 < NK:
                            s_mm(kt + 2)
                        ps_, psk = sb_list[kt]
                        pi = kt % 4
                        P.act(pT[pi][:], ps_[:, :], AF.Exp, [psk], ["pT%d" % pi])
                        P.mm(po[:, :], vaug[hb][:, kt, :], pT[pi][:], kt == 0, kt == NK - 1, [vk, "pT%d" % pi], [pok])
                    o = (h * 4 + qb) % 2
                    P.copy(OS[o][:], po[:, :], [pok], ["OS%d" % o], eng="scalar")
                    src = slice(64, 128) if hb == 0 else slice(0, 64)
                    dst = slice(0, 64) if hb == 0 else slice(64, 128)
                    P.recip(Rr[src, :], OS[o][src, :], ["OS%d" % o], ["Rr"])
                    pn, pnk = psr(0, 6, sctr)
                    P.mm(pn[:, :], shiftm[src, :], Rr[src, :], True, True, ["shiftm", "Rr"], [pnk])
                    c = h // 2
                    P.tt(ao[hb][dst, :], OS[o][dst, :], pn[dst, :], ALU.mult, ["OS%d" % o, pnk], ["ao%d" % hb])
                    P.dma("sync", a_dr[c, dst, qb * 512:(qb + 1) * 512], ao[hb][dst, :], reads=["ao%d" % hb], writes=["a_dr"])
            for h in range(H):
                load_head(h)
                compute_head(h)
            if debug:
                P.dma("sync", dbg["a"], a_dr, reads=["a_dr"])

        P.barrier()
        FSB = 1024

        def ffn(l, xT, sF):
            hT = sb("f_hT%d" % l, [128, 8, FSB], BF16, sF)
            AT = sb("f_AT%d" % l, [128, NFF, FSB], BF16, sF)
            w1b = [sb("f_w1b%d_%d" % (l, i), [128, 8, 128], BF16, sF) for i in range(2)]
            w3b = [sb("f_w3b%d_%d" % (l, i), [128, 8, 128], BF16, sF) for i in range(2)]
            w2b = [sb("f_w2b%d_%d" % (l, i), [128, NFF, 128], BF16, sF) for i in range(2)]
            sg = [sb("f_sg%d_%d" % (l, i), [128, 512], F32, sF) for i in range(2)]
            sq = sb("f_sq%d" % l, [128, 2, 512], BF16, sF)
            rs = sb("f_rs%d" % l, [128, 512], F32, sF)
            tmpf = [sb("f_tmpf%d_%d" % (l, i), [128, 512], F32, sF) for i in range(2)]
            n = 0
            ng = 0
            for sbk in range(NT // FSB):
                for hf in range(FSB // 512):
                    t0 = sbk * FSB + hf * 512
                    rstd, rk = pnorm([xT[:, fc, t0:t0 + 512] for fc in range(8)], ["xT"] * 8, D, 512, (sq, rs))
                    for fc in range(8):
                        o = fc % 2
                        P.stt(tmpf[o][:], xT[:, fc, t0:t0 + 512], A2[:, l, fc, 0:1], rstd, ALU.mult, ALU.mult, ["xT", rk, "A2"], ["f_tmpf%d" % o])
                        P.act(hT[:, fc, hf * 512:(hf + 1) * 512], tmpf[o][:], AF.Identity, ["f_tmpf%d" % o, "mod"], ["f_hT"], bias=mv(l, 3, fc, 0))
                for jf in range(NFF):
                    i = n % 2
                    n += 1
                    load_cast(w1b[i][:].rearrange("p a b -> p (a b)"), w1_d[l, jf], 1024, "f_w1b%d" % i)
                    load_cast(w3b[i][:].rearrange("p a b -> p (a b)"), w3_d[l, jf], 1024, "f_w3b%d" % i)
                    for hf in range(FSB // 512):
                        cs_ = slice(hf * 512, (hf + 1) * 512)
                        pg, pgk = psum()
                        for kc in range(8):
                            P.mm(pg[:, :], w1b[i][:, kc, :], hT[:, kc, cs_], kc == 0, kc == 7, ["f_w1b%d" % i, "f_hT"], [pgk])
                        pu, puk = psum()
                        for kc in range(8):
                            P.mm(pu[:, :], w3b[i][:, kc, :], hT[:, kc, cs_], kc == 0, kc == 7, ["f_w3b%d" % i, "f_hT"], [puk])
                        g_ = ng % 2
                        ng += 1
                        P.act(sg[g_][:], pg[:, :], AF.Silu, [pgk], ["f_sg%d" % g_])
                        P.tt(AT[:, jf, cs_], sg[g_][:], pu[:, :], ALU.mult, ["f_sg%d" % g_, puk], ["f_AT%d" % jf])
                atk = ["f_AT%d" % jf for jf in range(NFF)]
                for m in range(8):
                    i = n % 2
                    n += 1
                    load_cast(w2b[i][:].rearrange("p a b -> p (a b)"), w2_d[l, m], NFF * 128, "f_w2b%d" % i)
                    for hf in range(FSB // 512):
                        t0 = sbk * FSB + hf * 512
                        po, pok = psum()
                        for jf in range(NFF):
                            P.mm(po[:, :], w2b[i][:, jf, :], AT[:, jf, hf * 512:(hf + 1) * 512], jf == 0, jf == NFF - 1, ["f_w2b%d" % i, "f_AT%d" % jf], [pok])
                        P.stt(xT[:, m, t0:t0 + 512], po[:, :], mv(l, 5, m, 0), xT[:, m, t0:t0 + 512], ALU.mult, ALU.add, [pok, "mod", "xT"], ["xT"])

        def proj_res(l, wd, srcT, srck, xT, sP):
            wb = [sb("p_wb%d_%d" % (l, i), [128, 8, 128], BF16, sP) for i in range(2)]
            for m in range(8):
                i = m % 2
                load_cast(wb[i][:].rearrange("p a b -> p (a b)"), wd[m], 1024, "p_wb%d" % i)
                for blk in range(4):
                    po, pok = psum()
                    for kc in range(8):
                        P.mm(po[:, :], wb[i][:, kc, :], srcT[:, kc, blk * 512:(blk + 1) * 512], kc == 0, kc == 7, ["p_wb%d" % i, srck], [pok])
                    P.stt(xT[:, m, blk * 512:(blk + 1) * 512], po[:, :], mv(l, 2, m, 0), xT[:, m, blk * 512:(blk + 1) * 512], ALU.mult, ALU.add, [pok, "mod", "xT"], ["xT"])

        def load_xT(xT, rows_of_tile, sX):
            stage = sb("x_stage%d" % rows_of_tile[1], [128, D], F32, sX)
            for tt_ in range(NT // 128):
                P.dma("sync", stage[:], rows_of_tile[0](tt_), writes=["x_stage"])
                for g in range(2):
                    ps, pk = psum()
                    for f4 in range(4):
                        fc = g * 4 + f4
                        P.op("tensor", lambda e, fc=fc, f4=f4, ps=ps: e.transpose(ps[:, f4 * 128:(f4 + 1) * 128], stage[:, fc * 128:(fc + 1) * 128], ident[:]),
                             ["x_stage", "ident"], [pk])
                    P.copy(xT[:, g * 4:(g + 1) * 4, tt_ * 128:(tt_ + 1) * 128], ps[:].rearrange("p (f t) -> p f t", t=128), [pk], ["xT"], eng="scalar")

        with ExitStack() as sC:
            xT = sb("xT", [128, 8, NT], F32, sC)
            with ExitStack() as sC1:
                load_xT(xT, (lambda tt_: x_d[tt_ * 128:(tt_ + 1) * 128, :], 0), sC1)
                mixT = sb("mixT", [128, 8, NT], BF16, sC1)
                for c in range(4):
                    P.dma("sync", mixT[:, c, :], a_dr[c], reads=["a_dr"], writes=["mixT"])
                    P.dma("sync", mixT[:, 4 + c, :], s_dr[c, :, 1:NT + 1], reads=["s_dr"], writes=["mixT"])
                proj_res(0, wo_d, mixT, "mixT", xT, sC1)
            P.barrier()
            with ExitStack() as sC2:
                ffn(0, xT, sC2)
                sq = sb("c_sq", [128, 2, 512], BF16, sC2)
                rs = sb("c_rs", [128, 512], F32, sC2)
                tmpf = sb("c_tmpf", [128, 512], F32, sC2)
                ho = [sb("c_ho%d" % i, [128, 512], BF16, sC2) for i in range(2)]
                for blk in range(4):
                    t0 = blk * 512
                    rstd, rk = pnorm([xT[:, fc, t0:t0 + 512] for fc in range(8)], ["xT"] * 8, D, 512, (sq, rs))
                    for fc in range(8):
                        o = fc % 2
                        P.stt(tmpf[:], xT[:, fc, t0:t0 + 512], A1[:, 1, fc, 0:1], rstd, ALU.mult, ALU.mult, ["xT", rk, "A1"], ["c_tmpf"])
                        P.act(ho[o][:], tmpf[:], AF.Identity, ["c_tmpf", "mod"], ["c_ho%d" % o], bias=mv(1, 0, fc, 0))
                        P.dma("sync", h1_in.ap()[fc * 128:(fc + 1) * 128, t0:t0 + 512], ho[o][:], reads=["c_ho%d" % o], writes=["h1_in"])
                for fc in range(8):
                    P.dma("sync", x1_dr[fc], xT[:, fc, :], reads=["xT"], writes=["x1_dr"])
            if debug:
                P.dma("sync", dbg["x1"], x1_dr, reads=["x1_dr"])
                P.dma("sync", dbg["h1"], h1_in.ap(), reads=["h1_in"])
        P.barrier()
        P.coll(h1_in.ap(), h1_all.ap(), ["h1_in"], ["h1_all"])
        h1a = h1_all.ap()

        with ExitStack() as sD:
            XT = sb("XT", [128, 2, S], BF16, sD)
            cst = [sb("d_cst%d" % i, [128, NT], BF16, sD) for i in range(4)]
            dzer = sb("dzer", [128, NT], BF16, sD)
            P.memset(dzer[:], 0.0, ["dzer"])
            fcb = sb("fcb", [128, 2, 4, 128], BF16, sD)
            wab = sb("wab", [128, 2, 256], BF16, sD)
            mpb = sb("mpb", [64, 2, 128, 64], BF16, sD)
            Zsb = sb("Zsb", [128, 64, 128], BF16, sD)
            Ysb = sb("Ysb", [64, 64, 256], BF16, sD)
            Fsb = sb("Fsb", [64, 128, 64], BF16, sD)
            load_cast(fcb[:].rearrange("p a b c -> p (a b c)"), fc_d.rearrange("p a b c -> p (a b c)"), 1024, "fcb")
            load_cast(wab[:].rearrange("p a b -> p (a b)"), wa_d.rearrange("p a b -> p (a b)"), 512, "wab")
            mpf = mpb[:].rearrange("p a b c -> p (a b c)")
            mpd = mp_d.rearrange("p a b c -> p (a b c)")
            for i4 in range(4):
                load_cast(mpf[:, i4 * 4096:(i4 + 1) * 4096], mpd[:, i4 * 4096:(i4 + 1) * 4096], 4096, "mpb")
            nsel = 0
            for r in range(4):
                for cc in range(2):
                    dstX = XT[:, cc, r * NT:(r + 1) * NT]
                    for k in range(8):
                        bb_, gg = k // 4, k % 4
                        i = nsel % 4
                        nsel += 1
                        R = 4 * bb_ + r
                        row0 = R * D + gg * 256 + cc * 128
                        P.dma("sync", cst[i][:], h1a[row0:row0 + 128, :], reads=["h1_all"], writes=["d_cst%d" % i])
                        sel = vecs[:, V_SEL + k:V_SEL + k + 1]
                        P.stt(dstX, cst[i][:], sel, (dzer[:] if k == 0 else dstX), ALU.mult, ALU.add, ["d_cst%d" % i, "vecs", "XT%d%d" % (r, cc), "dzer"], ["XT%d%d" % (r, cc)])
            xkeys = ["XT%d%d" % (r, cc) for r in range(4) for cc in range(2)]
            fview = f_in.ap().rearrange("(q p) m -> q p m", p=128)
            ne = 0
            for mc in range(4):
                for b4 in range(16):
                    ps, pk = psum()
                    for bi in range(4):
                        bp = b4 * 4 + bi
                        for cc in range(2):
                            P.mm(ps[:, bi * 128:(bi + 1) * 128], XT[:, cc, bp::64], fcb[:, cc, mc, :], cc == 0, cc == 1, xkeys + ["fcb"], [pk])
                    ne += 1
                    P.copy(Zsb[:, b4 * 4:(b4 + 1) * 4, :], ps[:].rearrange("p (b m) -> p b m", m=128), [pk], ["Zsb"], eng="scalar" if ne % 2 else "vector")
                for m2 in range(32):
                    ps, pk = psum()
                    for mi in range(2):
                        m = m2 * 2 + mi
                        P.mm(ps[0:64, mi * 256:(mi + 1) * 256], Zsb[:, :, m], wab[:, 0, :], True, False, ["Zsb", "wab"], [pk])
                        P.mm(ps[0:64, mi * 256:(mi + 1) * 256], Zsb[:, :, 64 + m], wab[:, 1, :], False, True, ["Zsb", "wab"], [pk])
                    ne += 1
                    P.copy(Ysb[:, m2 * 2:(m2 + 1) * 2, :], ps[0:64, :].rearrange("p (m c) -> p m c", c=256), [pk], ["Ysb"], eng="scalar" if ne % 2 else "vector")
                for p8 in range(16):
                    ps, pk = psum()
                    for pi in range(8):
                        p = p8 * 8 + pi
                        P.mm(ps[0:64, pi * 64:(pi + 1) * 64], mpb[:, 0, p, :], Ysb[:, :, p], True, False, ["mpb", "Ysb"], [pk])
                        P.mm(ps[0:64, pi * 64:(pi + 1) * 64], mpb[:, 1, p, :], Ysb[:, :, 128 + p], False, True, ["mpb", "Ysb"], [pk])
                    ne += 1
                    P.copy(Fsb[:, p8 * 8:(p8 + 1) * 8, :], ps[0:64, :].rearrange("p (a m) -> p a m", m=64), [pk], ["Fsb"], eng="scalar" if ne % 2 else "vector")
                P.dma("sync", fview[:, :, mc * 64:(mc + 1) * 64], Fsb[:], reads=["Fsb"], writes=["f_in"])
            if debug:
                P.dma("sync", dbg["f"], f_in.ap(), reads=["f_in"])
        P.barrier()
        P.coll(f_in.ap(), f_all.ap(), ["f_in"], ["f_all"])
        fa = f_all.ap()

        with ExitStack() as sE:
            xT = sb("xT2", [128, 8, NT], F32, sE)
            for fc in range(8):
                P.dma("sync", xT[:, fc, :], x1_dr[fc], reads=["x1_dr"], writes=["xT"])
            with ExitStack() as sE1:
                FT = sb("FT", [128, 8, NT], BF16, sE1)
                est = [sb("e_st%d" % i, [128, 16, 256], BF16, sE1) for i in range(2)]
                facc = sb("e_acc", [128, 16, 256], F32, sE1)
                ezer = sb("ezer", [128, 16, 256], BF16, sE1)
                P.memset(ezer[:], 0.0, ["ezer"])
                nsel = 0
                for g in range(4):
                    for k in range(8):
                        bb_, jj = k // 4, k % 4
                        i = nsel % 2
                        nsel += 1
                        row0 = (4 * bb_ + g) * S + jj * NT
                        P.dma("sync", est[i][:], fa[row0:row0 + NT, :].rearrange("(t p) m -> p t m", p=128), reads=["f_all"], writes=["e_st%d" % i])
                        sel = vecs[:, V_SEL + k:V_SEL + k + 1]
                        P.stt(facc[:], est[i][:], sel, (ezer[:] if k == 0 else facc[:]), ALU.mult, ALU.add, ["e_st%d" % i, "vecs", "e_acc", "ezer"], ["e_acc"])
                    for tt_ in range(16):
                        ps, pk = psum()
                        for c2 in range(2):
                            P.op("tensor", lambda e, c2=c2, ps=ps, tt_=tt_: e.transpose(ps[:, c2 * 128:(c2 + 1) * 128], facc[:, tt_, c2 * 128:(c2 + 1) * 128], ident[:]),
                                 ["e_acc", "ident"], [pk])
                        P.copy(FT[:, g * 2:(g + 1) * 2, tt_ * 128:(tt_ + 1) * 128], ps[:, 0:256].rearrange("p (f t) -> p f t", t=128), [pk], ["FT"], eng="scalar")
                proj_res(1, wf_d, FT, "FT", xT, sE1)
            P.barrier()
            with ExitStack() as sE2:
                ffn(1, xT, sE2)
                ost = [sb("o_st%d" % i, [128, D], F32, sE2) for i in range(2)]
                for tt_ in range(NT // 128):
                    o = tt_ % 2
                    for g in range(2):
                        ps, pk = psum()
                        for f4 in range(4):
                            fc = g * 4 + f4
                            P.op("tensor", lambda e, fc=fc, f4=f4, ps=ps, tt_=tt_: e.transpose(ps[:, f4 * 128:(f4 + 1) * 128], xT[:, fc, tt_ * 128:(tt_ + 1) * 128], ident[:]),
                                 ["xT", "ident"], [pk])
                        P.copy(ost[o][:, g * 512:(g + 1) * 512], ps[:, :], [pk], ["o_st%d" % o], eng="scalar" if g else "vector")
                    P.dma("sync", out_d[tt_ * 128:(tt_ + 1) * 128, :], ost[o][:], reads=["o_st%d" % o], writes=["out"])

        P.wait("sync", [t for t in P.dlast if t is not None])
        P.emit()
    return nc


def _rope_tables(tok0):
    t = np.arange(tok0, tok0 + NT)
    row = (t // 64).astype(np.float32)
    col = (t % 64).astype(np.float32)
    half = 16
    inv = (1.0 / (10000.0 ** (np.arange(0, half, 2, dtype=np.float32) / half))).astype(np.float32)
    ar = row[:, None] * inv
    ac = col[:, None] * inv
    C = np.ones((96, NT), np.float32)
    Sn = np.zeros((96, NT), np.float32)
    for r in range(32):
        ang = ar if r < 16 else ac
        C[64 + r] = np.cos(ang[:, r % 8])
        Sn[64 + r] = np.sin(ang[:, r % 8])
    return C, Sn


def _consts():
    ident = np.eye(128, dtype=np.float32)
    pm = np.zeros((96, 96), np.float32)
    for r in range(32):
        d = 64 + r
        if (r % 16) < 8:
            pm[d, d + 8] = -1.0
        else:
            pm[d, d - 8] = 1.0
    pmT = np.ascontiguousarray(pm.T)
    shift = np.zeros((128, 128), np.float32)
    for k in range(128):
        shift[k, (k + 64) % 128] = 1.0
    c = np.arange(256)
    ang = 2 * np.pi * np.outer(c, c) / 256.0
    fcm = np.concatenate([np.cos(ang), -np.sin(ang)], 1).astype(np.float32)
    fcm = np.ascontiguousarray(fcm.reshape(2, 128, 2, 4, 64).transpose(1, 0, 3, 2, 4).reshape(128, 2, 4, 128))
    a = np.arange(128)
    ang = 2 * np.pi * np.outer(a, a) / 128.0
    Cm, Sm = np.cos(ang), np.sin(ang)
    wam = np.stack([np.concatenate([Cm, -Sm], 1), np.concatenate([Sm, Cm], 1)], 1).astype(np.float32)
    bq = np.arange(64)
    p = np.arange(128)
    ang = 2 * np.pi * (bq[:, None, None] * bq[None, None, :] / 64.0 + bq[:, None, None] * p[None, :, None] / 8192.0)
    sc = 1.0 / np.sqrt(8192.0 * 256.0)
    Mr = np.cos(ang) * sc
    Mi = -np.sin(ang) * sc
    mpm = np.stack([Mr, -Mi], 1).astype(np.float32)
    return dict(ident=ident, pmT=pmT, shiftm=shift, fcm=fcm, wam=wam, mpm=mpm)


def _prep_inputs(x, c, ctx, c_ctx, ada_w, ada_b, norm1_g, norm2_g, w_in, q_norm_g, kv_norm_g, w_uq, w_ukv,
                 q_gain, k_gain, conv_w, w_o, w_fourier, ffn_w1, ffn_w3, ffn_w2):
    f = lambda a: np.ascontiguousarray(np.asarray(a, dtype=np.float32))
    x, c, ctx, c_ctx = f(x), f(c), f(ctx), f(c_ctx)
    shared = _consts()
    shared["ada_w"] = f(np.asarray(ada_w).reshape(2, 8, 128, 12, 512).transpose(0, 3, 2, 1, 4).reshape(2, 12, 128, 8 * 512))
    shared["w_in"] = f(np.asarray(w_in)[0].reshape(8, 128, 2208).transpose(1, 0, 2))
    shared["w_uq"] = f(np.asarray(w_uq)[0].reshape(3, 128, 768).transpose(1, 0, 2))
    shared["w_ukv"] = f(np.asarray(w_ukv)[0].reshape(2, 128, 1024).transpose(1, 0, 2))
    colblk = lambda w, kc, nm: f(np.asarray(w).reshape(kc, 128, nm, 128).transpose(2, 1, 0, 3).reshape(nm, 128, kc * 128))
    shared["w_o"] = colblk(np.asarray(w_o)[0], 8, 8)
    shared["w_f"] = colblk(np.asarray(w_fourier)[0], 8, 8)
    shared["ffn_w1"] = np.stack([colblk(np.asarray(ffn_w1)[l], 8, NFF) for l in range(2)])
    shared["ffn_w3"] = np.stack([colblk(np.asarray(ffn_w3)[l], 8, NFF) for l in range(2)])
    shared["ffn_w2"] = np.stack([colblk(np.asarray(ffn_w2)[l], NFF, 8) for l in range(2)])
    in_maps = []
    for core in range(NCORES):
        b, j = core // 4, core % 4
        t0 = j * NT
        m = dict(shared)
        m["x"] = f(x[b, t0:t0 + NT])
        xh = np.zeros((2, D), np.float32)
        hm = np.zeros((2,), np.float32)
        if j > 0:
            xh[0] = x[b, t0 - 1]
            hm[0] = 1.0
        if j < 3:
            xh[1] = x[b, t0 + NT]
            hm[1] = 1.0
        m["xh"] = xh
        m["ctx"] = f(ctx[b])
        vecs = np.zeros((128, NVEC), np.float32)
        cs = np.stack([c[b], c_ctx], 0)
        vecs[:, V_CS:V_CS + 16] = cs.reshape(2, 8, 128).transpose(2, 1, 0).reshape(128, 16)
        for l in range(2):
            vecs[:, V_ADAB + 48 * l:V_ADAB + 48 * (l + 1)] = np.asarray(ada_b)[l].reshape(48, 128).T
            vecs[:, V_N1 + 8 * l:V_N1 + 8 * (l + 1)] = np.asarray(norm1_g)[l].reshape(8, 128).T
            vecs[:, V_N2 + 8 * l:V_N2 + 8 * (l + 1)] = np.asarray(norm2_g)[l].reshape(8, 128).T
        vecs[:, V_QNG:V_QNG + 3] = np.asarray(q_norm_g)[0].reshape(3, 128).T
        vecs[:, V_KVNG:V_KVNG + 2] = np.asarray(kv_norm_g)[0].reshape(2, 128).T
        vecs[0:96, V_QG] = np.asarray(q_gain)[0]
        vecs[0:96, V_KG] = np.asarray(k_gain)[0]
        vecs[:, V_CONV:V_CONV + 12] = np.asarray(conv_w)[0].reshape(3, 4, 128).transpose(2, 0, 1).reshape(128, 12)
        vecs[:, V_HM:V_HM + 2] = hm[None, :]
        vecs[:, V_MB + b] = 1.0
        vecs[:, V_SEL + core] = 1.0
        m["vecs"] = vecs
        C, Sn = _rope_tables(t0)
        m["ropec"], m["ropes"] = C, Sn
        in_maps.append(m)
    return in_maps


def kernel(**inputs):
    in_maps = _prep_inputs(**inputs)
    nc = build()
    res = run_bass_kernel_spmd(nc, in_maps, core_ids=list(range(NCORES)))
    out = np.zeros((NB, S, D), np.float32)
    for core in range(NCORES):
        b, j = core // 4, core % 4
        out[b, j * NT:(j + 1) * NT] = res.results[core]["out"]
    return out
```

```python
import numpy as np
from contextlib import ExitStack
import concourse.bass as bass
import concourse.mybir as mybir
from concourse.bass_utils import run_bass_kernel_spmd

F32 = mybir.dt.float32
BF16 = mybir.dt.bfloat16
AF = mybir.ActivationFunctionType
ALU = mybir.AluOpType

D = 1024
S = 8192
NB = 2
LCTX = 256
NT = 2048
H = 8
DFF = 2816
NFF = 22
EPS = 1e-6
QK_SCALE = 96 ** -0.5
NCORES = 8

V_CS = 0
V_ADAB = 16
V_N1 = 112
V_N2 = 128
V_QNG = 144
V_KVNG = 147
V_QG = 149
V_KG = 150
V_CONV = 151
V_HM = 163
V_MB = 165
V_SEL = 167
NVEC = 175


class Sem:
    def __init__(self, h):
        self.h = h
        self.count = 0


class Prog:
    ENG = ["sync", "scalar", "vector", "gpsimd", "tensor"]

    def __init__(self, nc, es, ndma=24):
        self.nc = nc
        self.q = {e: [] for e in self.ENG}
        self.esem = {e: Sem(es.enter_context(nc.semaphore("s_" + e))) for e in ["scalar", "vector", "gpsimd", "tensor"]}
        self.dsems = [Sem(es.enter_context(nc.semaphore("d%d" % i))) for i in range(ndma)]
        self.dlast = [None] * ndma
        self.dnext = 0
        self.waited = {e: {} for e in self.ENG}
        self.lastw = {}
        self.reads = {}
        self.nps = 0
        self.groups = {}

    def _x(self, keys):
        out = []
        for k in keys:
            out.extend(self.groups.get(k, []))
            out.append(k)
        return out

    def _deps(self, reads, writes):
        reads, writes = self._x(reads), self._x(writes)
        toks = []
        for k in reads:
            if k in self.lastw:
                toks.append(self.lastw[k])
        for k in writes:
            if k in self.lastw:
                toks.append(self.lastw[k])
            toks.extend(self.reads.get(k, []))
        return toks

    def _commit(self, tok, reads, writes):
        reads, writes = self._x(reads), self._x(writes)
        for k in writes:
            self.lastw[k] = tok
            self.reads[k] = []
        for k in reads:
            if k not in writes:
                self.reads.setdefault(k, []).append(tok)

    def _waits(self, eng, toks):
        out = []
        for t in toks:
            if t is None:
                continue
            sem, val, src = t
            if src == eng and eng == "tensor":
                continue
            w = self.waited[eng]
            if w.get(id(sem), 0) >= val:
                continue
            w[id(sem)] = val
            out.append((sem, val))
        return out

    def op(self, eng, fn, reads=(), writes=(), extra=()):
        toks = self._deps(reads, writes) + list(extra)
        waits = self._waits(eng, toks)
        sem = self.esem[eng]
        sem.count += 1
        tok = (sem, sem.count, eng)
        self.q[eng].append((waits, fn, sem, 1))
        self._commit(tok, reads, writes)
        return tok

    def dma(self, eng, out, in_, reads=(), writes=(), extra=(), **kw):
        i = self.dnext % len(self.dsems)
        self.dnext += 1
        sem = self.dsems[i]
        toks = self._deps(reads, writes) + list(extra) + [self.dlast[i]]
        waits = self._waits(eng, toks)
        sem.count += 16
        tok = (sem, sem.count, "dma")
        self.dlast[i] = tok
        self.q[eng].append((waits, lambda e: e.dma_start(out=out, in_=in_, **kw), sem, 16))
        self._commit(tok, reads, writes)
        return tok

    def coll(self, ins_ap, outs_ap, reads, writes):
        if not hasattr(self, "cc"):
            raise RuntimeError("no cc sem")
        toks = self._deps(reads, writes) + [self.cc_last]
        waits = self._waits("gpsimd", toks)
        self.cc.count += 1
        tok = (self.cc, self.cc.count, "cc")
        self.cc_last = tok
        self.q["gpsimd"].append((waits, lambda e: e.collective_compute("AllGather", ALU.bypass, replica_groups=[list(range(NCORES))], ins=[ins_ap], outs=[outs_ap]), self.cc, 1))
        self._commit(tok, reads, writes)
        return tok

    def barrier(self):
        toks = [(sm, sm.count, name) for name, sm in self.esem.items() if sm.count > 0]
        toks += [t for t in self.dlast if t is not None]
        if self.cc_last is not None:
            toks.append(self.cc_last)
        for e in self.ENG:
            self.wait(e, toks)

    def wait(self, eng, toks):
        waits = self._waits(eng, toks)
        if waits:
            self.q[eng].append((waits, None, None, 0))

    def emit(self):
        nc = self.nc
        with nc.Block() as block:
            def mk(name):
                def body(e):
                    for waits, fn, sem, amt in self.q[name]:
                        for s, v in waits:
                            e.wait_ge(s.h, v)
                        if fn is not None:
                            ins = fn(e)
                            if sem is not None:
                                ins.then_inc(sem.h, amt)
                return body
            block.sync(mk("sync"))
            block.scalar(mk("scalar"))
            block.vector(mk("vector"))
            block.gpsimd(mk("gpsimd"))
            block.tensor(mk("tensor"))

    def mm(self, out, lhsT, rhs, start, stop, reads, writes):
        return self.op("tensor", lambda e: e.matmul(out, lhsT, rhs, start=start, stop=stop), reads, writes)

    def act(self, out, in_, func, reads, writes, eng="scalar", **kw):
        return self.op(eng, lambda e: e.activation(out=out, in_=in_, func=func, **kw), reads, writes)

    def tt(self, out, in0, in1, op, reads, writes, eng="vector"):
        return self.op(eng, lambda e: e.tensor_tensor(out=out, in0=in0, in1=in1, op=op), reads, writes)

    def stt(self, out, in0, scalar, in1, op0, op1, reads, writes, eng="vector"):
        if eng == "gpsimd":
            self.ts(in0, in0, scalar, None, op0, None, reads, [reads[0]], eng=eng)
            return self.tt(out, in0, in1, op1, reads, writes, eng=eng)
        return self.op(eng, lambda e: e.scalar_tensor_tensor(out=out, in0=in0, scalar=scalar, in1=in1, op0=op0, op1=op1), reads, writes)

    def ts(self, out, in0, s1, s2, op0, op1, reads, writes, eng="vector"):
        if s2 is None:
            return self.op(eng, lambda e: e.tensor_scalar(out=out, in0=in0, scalar1=s1, scalar2=None, op0=op0), reads, writes)
        return self.op(eng, lambda e: e.tensor_scalar(out=out, in0=in0, scalar1=s1, scalar2=s2, op0=op0, op1=op1), reads, writes)

    def copy(self, out, in_, reads, writes, eng="vector"):
        if eng == "scalar":
            return self.op(eng, lambda e: e.activation(out=out, in_=in_, func=AF.Copy), reads, writes)
        return self.op(eng, lambda e: e.tensor_copy(out=out, in_=in_), reads, writes)

    def memset(self, ap, val, writes, eng="vector"):
        return self.op(eng, lambda e: e.memset(ap, val), (), writes)

    def recip(self, out, in_, reads, writes):
        return self.op("vector", lambda e: e.reciprocal(out=out, in_=in_), reads, writes)


def build(debug=False, stop_after=None):
    nc = bass.Bass("TRN2", target_bir_lowering=False)

    def din(name, shape, dt=F32):
        return nc.dram_tensor(name, list(shape), dt, kind="ExternalInput").ap()

    def dout(name, shape, dt=F32):
        return nc.dram_tensor(name, list(shape), dt, kind="ExternalOutput").ap()

    x_d = din("x", [NT, D])
    xh_d = din("xh", [2, D])
    ctx_d = din("ctx", [LCTX, D])
    vecs_d = din("vecs", [128, NVEC])
    adaw_d = din("ada_w", [2, 12, 128, 8 * 512])
    win_d = din("w_in", [128, 8, 2208])
    wuq_d = din("w_uq", [128, 3, 768])
    wukv_d = din("w_ukv", [128, 2, 1024])
    wo_d = din("w_o", [8, 128, 8 * 128])
    wf_d = din("w_f", [8, 128, 8 * 128])
    w1_d = din("ffn_w1", [2, NFF, 128, 8 * 128])
    w3_d = din("ffn_w3", [2, NFF, 128, 8 * 128])
    w2_d = din("ffn_w2", [2, 8, 128, NFF * 128])
    ident_d = din("ident", [128, 128])
    ropec_d = din("ropec", [96, NT])
    ropes_d = din("ropes", [96, NT])
    pmT_d = din("pmT", [96, 96])
    shift_d = din("shiftm", [128, 128])
    fc_d = din("fcm", [128, 2, 4, 128])
    wa_d = din("wam", [128, 2, 256])
    mp_d = din("mpm", [64, 2, 128, 64])
    out_d = dout("out", [NT, D])

    q_dr = nc.dram_tensor("q_dr", [H, 96, NT], BF16).ap()
    s_dr = nc.dram_tensor("s_dr", [4, 128, NT + 1], BF16).ap()
    kc_dr = nc.dram_tensor("kc_dr", [H, 96, LCTX], BF16).ap()
    vc_dr = nc.dram_tensor("vc_dr", [LCTX, 512], BF16).ap()
    kv_in = nc.dram_tensor("kv_in", [1280, NT], BF16)
    kv_all = nc.dram_tensor("kv_all", [NCORES * 1280, NT], BF16)
    h1_in = nc.dram_tensor("h1_in", [D, NT], BF16)
    h1_all = nc.dram_tensor("h1_all", [NCORES * D, NT], BF16)
    f_in = nc.dram_tensor("f_in", [S, 256], BF16)
    f_all = nc.dram_tensor("f_all", [NCORES * S, 256], BF16)
    x1_dr = nc.dram_tensor("x1_dr", [8, 128, NT], F32).ap()

    dbg = {}
    if debug:
        dbg["q"] = dout("dbg_q", [H, 96, NT], BF16)
        dbg["kv"] = dout("dbg_kv", [1280, NT], BF16)
        dbg["s"] = dout("dbg_s", [4, 128, NT + 1], BF16)
        dbg["kc"] = dout("dbg_kc", [H, 96, LCTX], BF16)
        dbg["vc"] = dout("dbg_vc", [LCTX, 512], BF16)
        dbg["mod"] = dout("dbg_mod", [128, 2 * 96])
        dbg["x1"] = dout("dbg_x1", [8, 128, NT])
        dbg["h1"] = dout("dbg_h1", [D, NT], BF16)
        dbg["f"] = dout("dbg_f", [S, 256], BF16)
        dbg["a"] = dout("dbg_a", [4, 128, NT], BF16)

    with ExitStack() as es:
        P = Prog(nc, es)
        P.cc = Sem(es.enter_context(nc.semaphore("cc")))
        P.cc_last = None

        def sb(name, shape, dt=F32, stack=es):
            return stack.enter_context(nc.sbuf_tensor("t_" + name, list(shape), dt))

        psb = [es.enter_context(nc.psum_tensor("ps%d" % i, [128, 512], F32)) for i in range(8)]

        def psum():
            i = P.nps % 8
            P.nps += 1
            return psb[i], "ps%d" % i

        wst = [sb("wst%d" % i, [128, 4096]) for i in range(2)]
        nst = [0]

        def load_cast(dst, src_d, ncols, dkey):
            i = nst[0] % 2
            nst[0] += 1
            Pn = dst.shape[0]
            P.dma("sync", wst[i][0:Pn, 0:ncols], src_d, writes=["wst%d" % i])
            if ncols < 512:
                P.copy(dst, wst[i][0:Pn, 0:ncols], ["wst%d" % i], [dkey], eng="vector")
            else:
                P.groups[dkey] = [dkey + "_a", dkey + "_b", dkey + "_c"]
                c1 = (ncols * 14 // 100) // 8 * 8
                c2 = c1 + (ncols * 46 // 100) // 8 * 8
                P.copy(dst[:, 0:c1], wst[i][0:Pn, 0:c1], ["wst%d" % i], [dkey + "_a"], eng="gpsimd")
                P.copy(dst[:, c1:c2], wst[i][0:Pn, c1:c2], ["wst%d" % i], [dkey + "_b"], eng="vector")
                P.copy(dst[:, c2:ncols], wst[i][0:Pn, c2:ncols], ["wst%d" % i], [dkey + "_c"], eng="scalar")

        vecs = sb("vecs", [128, NVEC])
        ident = sb("ident", [128, 128])
        identb = sb("identb", [128, 128], BF16)
        onesb = sb("onesb", [128, 128], BF16)
        mod = sb("mod", [128, 2, 48, 2])
        A1 = sb("A1", [128, 2, 8, 2])
        A2 = sb("A2", [128, 2, 8, 2])
        gsc = sb("gsc", [128, 2])
        P.dma("sync", vecs[:], vecs_d, writes=["vecs"])
        P.dma("sync", ident[:], ident_d, writes=["ident"])
        P.copy(identb[:], ident[:], ["ident"], ["identb"])
        P.memset(onesb[:], 1.0, ["onesb"])

        def pnorm(chunks, keys, nfeat, N, tmp_pool, sfx=""):
            Pi = chunks[0].shape[0]
            sq, rs = tmp_pool
            ps, pk = psum()
            for i, (c, k) in enumerate(zip(chunks, keys)):
                sqi = sq[0:Pi, i % 2, 0:N]
                P.act(sqi, c, AF.Square, [k], ["sq%d" % (i % 2) + sfx])
                P.mm(ps[0:Pi, 0:N], onesb[0:Pi, 0:Pi], sqi, i == 0, i == len(chunks) - 1, ["sq%d" % (i % 2) + sfx, "onesb"], [pk])
            P.act(rs[0:Pi, 0:N], ps[0:Pi, 0:N], AF.Ln, [pk], ["rs" + sfx], scale=1.0 / nfeat, bias=EPS)
            P.act(rs[0:Pi, 0:N], rs[0:Pi, 0:N], AF.Exp, ["rs" + sfx], ["rs" + sfx], scale=-0.5)
            return rs[0:Pi, 0:N], "rs" + sfx

        with ExitStack() as s0:
            csb = sb("csb", [128, 16], BF16, s0)
            wts = [sb("adawt%d" % i, [128, 8, 512], BF16, s0) for i in range(2)]
            P.act(csb[:], vecs[:, V_CS:V_CS + 16], AF.Silu, ["vecs"], ["csb"])
            n = 0
            for l in range(2):
                ps, pk = psum()
                for nb in range(12):
                    wt = wts[n % 2]
                    wk_ = "adawt%d" % (n % 2)
                    n += 1
                    load_cast(wt[:].rearrange("p a b -> p (a b)"), adaw_d[l, nb], 4096, wk_)
                    for jj in range(4):
                        j = nb * 4 + jj
                        for kc in range(8):
                            P.mm(ps[:, 2 * j:2 * j + 2], wt[:, kc, jj * 128:(jj + 1) * 128], csb[:, 2 * kc:2 * kc + 2],
                                 kc == 0, kc == 7, [wk_, "csb"], [pk])
                P.tt(mod[:, l], ps[:, 0:96].rearrange("p (a b) -> p a b", b=2),
                     vecs[:, V_ADAB + 48 * l:V_ADAB + 48 * (l + 1)].unsqueeze(2).to_broadcast([128, 48, 2]),
                     ALU.add, [pk, "vecs"], ["mod"])
                P.stt(A1[:, l], mod[:, l, 8:16, :], 1.0, vecs[:, V_N1 + 8 * l:V_N1 + 8 * (l + 1)].unsqueeze(2).to_broadcast([128, 8, 2]),
                      ALU.add, ALU.mult, ["mod", "vecs"], ["A1"])
                P.stt(A2[:, l], mod[:, l, 32:40, :], 1.0, vecs[:, V_N2 + 8 * l:V_N2 + 8 * (l + 1)].unsqueeze(2).to_broadcast([128, 8, 2]),
                      ALU.add, ALU.mult, ["mod", "vecs"], ["A2"])
            P.act(gsc[:, 0:1], vecs[:, V_QG:V_QG + 1], AF.Copy, ["vecs"], ["gsc"], scale=QK_SCALE)
            P.act(gsc[:, 1:2], vecs[:, V_KG:V_KG + 1], AF.Copy, ["vecs"], ["gsc"])
            if debug:
                P.dma("sync", dbg["mod"], mod[:].rearrange("p l a b -> p (l a b)"), reads=["mod"])
        P.barrier()
        def mv(l, idx, fc, j):
            return mod[:, l, idx * 8 + fc, j:j + 1]

        with ExitStack() as sa:
            winb = sb("winb", [128, 8, 2208], BF16, sa)
            wpe = sb("wpe", [128, 8, 96], BF16, sa)
            wuqb = sb("wuqb", [128, 3, 768], BF16, sa)
            wukvb = sb("wukvb", [128, 2, 1024], BF16, sa)
            wk = sb("wk", [128, 2, 8, 96], BF16, sa)
            wv = sb("wv", [128, 2, 8, 64], BF16, sa)
            ropec = sb("ropec", [96, NT], F32, sa)
            ropes = sb("ropes", [96, NT], F32, sa)
            pmT = sb("pmT", [96, 96], BF16, sa)
            xTb = sb("xTb", [128, 8, 512], F32, sa)
            stage = sb("stage", [128, D], F32, sa)
            hT = sb("hT", [128, 8, 512], BF16, sa)
            sq = sb("sq", [128, 2, 512], BF16, sa)
            rs = sb("rs", [128, 512], F32, sa)
            tmpf = sb("tmpf", [128, 512], F32, sa)
            cqf = sb("cqf", [128, 5, 512], F32, sa)
            cqn = sb("cqn", [128, 5, 512], BF16, sa)
            qns = [sb("qn%d" % i, [96, 512], BF16, sa) for i in range(4)]
            t1s = [sb("t1_%d" % i, [96, 512], F32, sa) for i in range(4)]
            sq4 = [sb("sq4_%d" % i, [96, 512], BF16, sa) for i in range(4)]
            rs4 = [sb("rs4_%d" % i, [96, 512], F32, sa) for i in range(4)]
            qo4 = sb("qo4", [96, 4, 512], BF16, sa)
            vst = sb("vst", [128, 2, 512], BF16, sa)
            zb = sb("zb", [128, 4, 514], F32, sa)
            bb = sb("bb", [128, 4, 513], F32, sa)
            uf = sb("uf", [128, 512], F32, sa)
            yt = sb("yt", [128, 512], F32, sa)
            so = sb("so", [128, 2, 512], BF16, sa)

            for kc in range(8):
                load_cast(winb[:, kc, :], win_d[:, kc, :], 2208, "winb")
            load_cast(wuqb[:].rearrange("p a b -> p (a b)"), wuq_d.rearrange("p a b -> p (a b)"), 3 * 768, "wuqb")
            load_cast(wukvb[:].rearrange("p a b -> p (a b)"), wukv_d.rearrange("p a b -> p (a b)"), 2048, "wukvb")
            load_cast(pmT[:], pmT_d, 96, "pmT")
            P.dma("sync", ropec[:], ropec_d, writes=["ropec"])
            P.dma("sync", ropes[:], ropes_d, writes=["ropes"])
            P.memset(wpe[:], 0.0, ["wpe"])
            P.memset(wk[:], 0.0, ["wk"])
            P.memset(zb[:], 0.0, ["zb"])
            P.memset(bb[:], 0.0, ["bb"])
            P.copy(wpe[:, :, 64:96], winb[:, :, 640:672], ["winb"], ["wpe"])
            wukv4 = wukvb[:].rearrange("p k (h d) -> p k h d", d=128)
            P.copy(wk[:, :, :, 0:64], wukv4[:, :, :, 0:64], ["wukvb"], ["wk"])
            P.copy(wv[:], wukv4[:, :, :, 64:128], ["wukvb"], ["wv"])

            nq = [0]

            def load_T(rows_ap, ntok, col0):
                P.dma("sync", stage[0:ntok, :], rows_ap, writes=["stage"])
                for g in range(2):
                    ps, pk = psum()
                    for f4 in range(4):
                        fc = g * 4 + f4
                        P.op("tensor", lambda e, fc=fc, f4=f4, ps=ps: e.transpose(ps[:, f4 * 128:f4 * 128 + ntok], stage[0:ntok, fc * 128:(fc + 1) * 128], ident[0:ntok, 0:ntok]),
                             ["stage", "ident"], [pk])
                    P.copy(xTb[:, g * 4:(g + 1) * 4, col0:col0 + ntok],
                           ps[:].rearrange("p (f t) -> p f t", t=128)[:, :, 0:ntok], [pk], ["xTb"], eng="scalar")

            def rms_mod(src, srck, N, Asc, Bsh, dst, dstk):
                rstd, rk = pnorm([src[:, fc, 0:N] for fc in range(8)], [srck] * 8, D, N, (sq, rs))
                for fc in range(8):
                    P.stt(tmpf[:, 0:N], src[:, fc, 0:N], Asc(fc), rstd, ALU.mult, ALU.mult, [srck, rk, "A1", "A2"], ["tmpf"])
                    P.act(dst[:, fc, 0:N], tmpf[:, 0:N], AF.Identity, ["tmpf", "mod"], [dstk], bias=Bsh(fc))

            def lin_in(col0, ncols, N, wsrc=None):
                ps, pk = psum()
                for kc in range(8):
                    P.mm(ps[0:ncols, 0:N], winb[:, kc, col0:col0 + ncols], hT[:, kc, 0:N], kc == 0, kc == 7, ["winb", "hT"], [pk])
                return ps, pk

            def lora_norm(col0, nch, gcol, N, base):
                for i in range(nch):
                    ps, pk = lin_in(col0 + 128 * i, 128, N)
                    P.copy(cqf[:, base + i, 0:N], ps[:, 0:N], [pk], ["cqf%d" % base], eng="scalar")
                rstd, rk = pnorm([cqf[:, base + i, 0:N] for i in range(nch)], ["cqf%d" % base] * nch, nch * 128, N, (sq, rs))
                for i in range(nch):
                    P.stt(cqn[:, base + i, 0:N], cqf[:, base + i, 0:N], vecs[:, gcol + i:gcol + i + 1], rstd, ALU.mult, ALU.mult,
                          ["cqf%d" % base, rk, "vecs"], ["cqn%d" % base])

            def head_stage(heads, raw_mm, N, gcol, tok0, rope, dst_fn):
                G = len(heads)
                pss = []
                for gi, h in enumerate(heads):
                    ps, pk = psum()
                    raw_mm(h, ps, pk)
                    pss.append((ps, pk))
                for gi in range(G):
                    ps, pk = pss[gi]
                    P.act(sq4[gi][0:96, 0:N], ps[0:96, 0:N], AF.Square, [pk], ["sqh%d" % gi])
                for gi in range(G):
                    ps, pk = pss[gi]
                    P.copy(t1s[gi][:, 0:N], ps[0:96, 0:N], [pk], ["t1_%d" % gi], eng="scalar")
                ps2 = []
                for gi in range(G):
                    p2, k2 = psum()
                    P.mm(p2[0:96, 0:N], onesb[0:96, 0:96], sq4[gi][0:96, 0:N], True, True, ["sqh%d" % gi, "onesb"], [k2])
                    ps2.append((p2, k2))
                for gi in range(G):
                    p2, k2 = ps2[gi]
                    P.act(rs4[gi][0:96, 0:N], p2[0:96, 0:N], AF.Ln, [k2], ["rsh%d" % gi], scale=1.0 / 96, bias=EPS)
                for gi in range(G):
                    P.act(rs4[gi][0:96, 0:N], rs4[gi][0:96, 0:N], AF.Exp, ["rsh%d" % gi], ["rsh%d" % gi], scale=-0.5)
                if not rope:
                    for gi in range(G):
                        P.stt(qo4[:, gi, 0:N], t1s[gi][:, 0:N], gsc[0:96, gcol:gcol + 1], rs4[gi][0:96, 0:N], ALU.mult, ALU.mult,
                              ["t1_%d" % gi, "rsh%d" % gi, "gsc"], ["qo4_%d" % gi])
                else:
                    for gi in range(G):
                        P.stt(qns[gi][:, 0:N], t1s[gi][:, 0:N], gsc[0:96, gcol:gcol + 1], rs4[gi][0:96, 0:N], ALU.mult, ALU.mult,
                              ["t1_%d" % gi, "rsh%d" % gi, "gsc"], ["qn_%d" % gi])
                    ps3 = []
                    for gi in range(G):
                        p3, k3 = psum()
                        P.mm(p3[0:96, 0:N], pmT[:], qns[gi][:, 0:N], True, True, ["pmT", "qn_%d" % gi], [k3])
                        ps3.append((p3, k3))
                    for gi in range(G):
                        p3, k3 = ps3[gi]
                        P.tt(rs4[gi][0:96, 0:N], p3[0:96, 0:N], ropes[:, tok0:tok0 + N], ALU.mult, [k3, "ropes"], ["rsh%d" % gi])
                    for gi in range(G):
                        P.tt(t1s[gi][:, 0:N], qns[gi][:, 0:N], ropec[:, tok0:tok0 + N], ALU.mult, ["qn_%d" % gi, "ropec"], ["t1_%d" % gi])
                    for gi in range(G):
                        P.tt(qo4[:, gi, 0:N], t1s[gi][:, 0:N], rs4[gi][0:96, 0:N], ALU.add, ["t1_%d" % gi, "rsh%d" % gi], ["qo4_%d" % gi])
                for gi, h in enumerate(heads):
                    dst, dk = dst_fn(h)
                    P.dma("sync", dst, qo4[:, gi, 0:N], reads=["qo4_%d" % gi], writes=[dk])

            def proc(kind, N, tok0):
                j = 1 if kind == "ctx" else 0
                rms_mod(xTb, "xTb", N, lambda fc: A1[:, 0, fc, j:j + 1], lambda fc: mv(0, 0, fc, j), hT, "hT")
                if kind == "lat":
                    lora_norm(0, 3, V_QNG, N, 0)
                    def q_raw(h, ps, pk):
                        for kc in range(3):
                            P.mm(ps[0:96, 0:N], wuqb[:, kc, h * 96:(h + 1) * 96], cqn[:, kc, 0:N], kc == 0, kc == 2, ["wuqb", "cqn0"], [pk])
                    for g4 in range(2):
                        head_stage(list(range(g4 * 4, g4 * 4 + 4)), q_raw, N, 0, tok0, True, lambda h: (q_dr[h, :, tok0:tok0 + N], "q_dr"))
                if kind in ("lat", "ctx"):
                    lora_norm(384, 2, V_KVNG, N, 3)
                    def k_raw(h, ps, pk):
                        for kc in range(2):
                            P.mm(ps[0:96, 0:N], wk[:, kc, h, :], cqn[:, 3 + kc, 0:N], kc == 0, False, ["wk", "cqn3"], [pk])
                        for kc in range(8):
                            P.mm(ps[0:96, 0:N], wpe[:, kc, :], hT[:, kc, 0:N], False, kc == 7, ["wpe", "hT"], [pk])
                    if kind == "lat":
                        kdst = lambda h: (kv_in.ap()[h * 96:(h + 1) * 96, tok0:tok0 + N], "kv_in")
                    else:
                        kdst = lambda h: (kc_dr[h, :, :], "kc_dr")
                    for g4 in range(2):
                        head_stage(list(range(g4 * 4, g4 * 4 + 4)), k_raw, N, 1, tok0, kind == "lat", kdst)
                    for tt_ in range(N // 128):
                        ps, pk = psum()
                        for kc in range(2):
                            P.mm(ps[:, :], cqn[:, 3 + kc, tt_ * 128:(tt_ + 1) * 128], wv[:, kc].rearrange("p h d -> p (h d)"),
                                 kc == 0, kc == 1, ["wv", "cqn3"], [pk])
                        o = nq[0] % 2
                        nq[0] += 1
                        P.copy(vst[:, o, :], ps[:, :], [pk], ["vst%d" % o], eng="scalar")
                        if kind == "lat":
                            t0 = tok0 + tt_ * 128
                            dst = kv_in.ap()[768 + t0 // 4:768 + t0 // 4 + 32, :].rearrange("r (q c) -> (r q) c", c=512)
                            P.dma("sync", dst, vst[:, o, :], reads=["vst%d" % o], writes=["kv_in"])
                        else:
                            P.dma("sync", vc_dr[tt_ * 128:(tt_ + 1) * 128, :], vst[:, o, :], reads=["vst%d" % o], writes=["vc_dr"])
                if kind == "lat":
                    for ch in range(4):
                        psb_, pkb = lin_in(672 + 128 * ch, 128, N)
                        P.copy(bb[:, ch, 1:1 + N], psb_[:, 0:N], [pkb], ["bb"], eng="scalar")
                        psu, pku = lin_in(1696 + 128 * ch, 128, N)
                        P.copy(uf[:, 0:N], psu[:, 0:N], [pku], ["uf"], eng="scalar")
                        psc, pkc = lin_in(1184 + 128 * ch, 128, N)
                        P.tt(zb[:, ch, 2:2 + N], psc[:, 0:N], uf[:, 0:N], ALU.mult, [pkc, "uf"], ["zb"])
                    conv_out(N, tok0)
                if kind == "halo":
                    for ch in range(4):
                        psu, pku = lin_in(1696 + 128 * ch, 128, N)
                        P.copy(uf[:, 0:N], psu[:, 0:N], [pku], ["uf"], eng="scalar")
                        psc, pkc = lin_in(1184 + 128 * ch, 128, N)
                        P.tt(uf[:, 0:N], psc[:, 0:N], uf[:, 0:N], ALU.mult, [pkc, "uf"], ["uf"])
                        P.tt(zh[:, ch, :], uf[:, 0:2], vecs[:, V_HM:V_HM + 2], ALU.mult, ["uf", "vecs"], ["zh"])

            def conv_out(N, tok0):
                for ch in range(4):
                    P.act(yt[:, 0:N], zb[:, ch, 0:N], AF.Copy, ["zb", "vecs"], ["yt"], scale=vecs[:, V_CONV + ch:V_CONV + ch + 1])
                    P.stt(yt[:, 0:N], zb[:, ch, 1:1 + N], vecs[:, V_CONV + 4 + ch:V_CONV + 5 + ch], yt[:, 0:N], ALU.mult, ALU.add, ["zb", "vecs", "yt"], ["yt"])
                    P.stt(yt[:, 0:N], zb[:, ch, 2:2 + N], vecs[:, V_CONV + 8 + ch:V_CONV + 9 + ch], yt[:, 0:N], ALU.mult, ALU.add, ["zb", "vecs", "yt"], ["yt"])
                    o = nq[0] % 2
                    nq[0] += 1
                    P.tt(so[:, o, 0:N], yt[:, 0:N], bb[:, ch, 0:N], ALU.mult, ["yt", "bb"], ["so%d" % o])
                    P.dma("sync", s_dr[ch, :, tok0:tok0 + N], so[:, o, 0:N], reads=["so%d" % o], writes=["s_dr"], allow_slow_non_contiguous=True)
                P.copy(zb[:, :, 0:2], zb[:, :, N:N + 2], ["zb"], ["zb"])
                P.copy(bb[:, :, 0:1], bb[:, :, N:N + 1], ["bb"], ["bb"])

            zh = sb("zh", [128, 4, 2], F32, sa)
            load_T(xh_d, 2, 0)
            proc("halo", 2, 0)
            P.copy(zb[:, :, 1:2], zh[:, :, 0:1], ["zh", "zb"], ["zb"])
            for tt_ in range(2):
                load_T(ctx_d[tt_ * 128:(tt_ + 1) * 128, :], 128, tt_ * 128)
            proc("ctx", 256, 0)
            for blk in range(4):
                for tt_ in range(4):
                    load_T(x_d[blk * 512 + tt_ * 128:blk * 512 + (tt_ + 1) * 128, :], 128, tt_ * 128)
                proc("lat", 512, blk * 512)
            P.copy(zb[:, :, 2:3], zh[:, :, 1:2], ["zh", "zb"], ["zb"])
            conv_out(1, NT)

            if debug:
                P.dma("sync", dbg["q"], q_dr, reads=["q_dr"])
                P.dma("sync", dbg["kv"], kv_in.ap(), reads=["kv_in"])
                P.dma("sync", dbg["s"], s_dr, reads=["s_dr"])
                P.dma("sync", dbg["kc"], kc_dr, reads=["kc_dr"])
                P.dma("sync", dbg["vc"], vc_dr, reads=["vc_dr"])


        P.barrier()

        def psr(lo, hi, ctr):
            i = lo + ctr[0] % (hi - lo)
            ctr[0] += 1
            return psb[i], "ps%d" % i

        a_dr = nc.dram_tensor("a_dr", [4, 128, NT], BF16).ap()
        P.coll(kv_in.ap(), kv_all.ap(), ["kv_in"], ["kv_all"])
        kva = kv_all.ap()

        with ExitStack() as sB:
            kT = [sb("kT%d" % i, [96, S + LCTX], BF16, sB) for i in range(2)]
            vaug = [sb("vaug%d" % i, [128, 66, 128], BF16, sB) for i in range(2)]
            kst = [sb("kst%d" % i, [96, 2, NT], BF16, sB) for i in range(2)]
            vsg = [sb("vsg%d" % i, [128, 2, 16, 64], BF16, sB) for i in range(2)]
            qh = [sb("qh%d" % i, [96, NT], BF16, sB) for i in range(2)]
            pT = [sb("pT%d" % i, [128, 512], BF16, sB) for i in range(4)]
            OS = [sb("OS%d" % i, [128, 512], F32, sB) for i in range(2)]
            Rr = sb("Rr", [128, 512], F32, sB)
            ao = [sb("ao%d" % i, [128, 512], BF16, sB) for i in range(2)]
            shiftm = sb("shiftm", [128, 128], F32, sB)
            P.dma("sync", shiftm[:], shift_d, writes=["shiftm"])
            for i in range(2):
                P.memset(vaug[i][:], 1.0, ["vaug%d" % i], eng="gpsimd")
            m0 = vecs[:, V_MB:V_MB + 1]
            m1 = vecs[:, V_MB + 1:V_MB + 2]
            nsel = [0]
            sctr = [0]
            octr = [0]
            zer = sb("zer", [128, NT], BF16, sB)
            P.memset(zer[:], 0.0, ["zer"])

            def load_head(h):
                hb = h % 2
                off = 0 if hb == 0 else 64
                kk, vk, qk = "kT%d" % hb, "vaug%d" % hb, "qh%d" % hb
                P.dma("sync", qh[hb][:], q_dr[h], reads=["q_dr"], writes=[qk])
                for r in range(4):
                    i = nsel[0] % 2
                    nsel[0] += 1
                    for cb in range(2):
                        R = 4 * cb + r
                        P.dma("sync", kst[i][:, cb, :], kva[R * 1280 + h * 96:R * 1280 + (h + 1) * 96, :], reads=["kv_all"], writes=["kst%d_%d" % (i, cb)])
                        vsrc = kva[R * 1280 + 768:R * 1280 + 1280, :].rearrange("r (q c) -> (r q) c", c=512).rearrange("(t p) c -> p t c", p=128)[:, :, h * 64:(h + 1) * 64]
                        P.dma("sync", vsg[i][:, cb], vsrc, reads=["kv_all"], writes=["vsg%d_%d" % (i, cb)])
                    kdst = kT[hb][:, r * NT:(r + 1) * NT]
                    P.stt(kdst, kst[i][:, 0, :], m0[0:96], zer[0:96, :], ALU.mult, ALU.add, ["kst%d_0" % i, "vecs", "zer"], [kk])
                    P.stt(kdst, kst[i][:, 1, :], m1[0:96], kdst, ALU.mult, ALU.add, ["kst%d_1" % i, "vecs", kk], [kk])
                    vdst = vaug[hb][:, r * 16:(r + 1) * 16, off:off + 64]
                    zv = zer[:, 0:1024].rearrange("p (t c) -> p t c", c=64)
                    P.stt(vdst, vsg[i][:, 0], m0, zv, ALU.mult, ALU.add, ["vsg%d_0" % i, "vecs", "zer"], [vk])
                    P.stt(vdst, vsg[i][:, 1], m1, vdst, ALU.mult, ALU.add, ["vsg%d_1" % i, "vecs", vk], [vk])
                P.dma("sync", kT[hb][:, S:S + LCTX], kc_dr[h], reads=["kc_dr"], writes=[kk])
                P.dma("sync", vaug[hb][:, 64:66, off:off + 64], vc_dr.rearrange("(t p) c -> p t c", p=128)[:, :, h * 64:(h + 1) * 64], reads=["vc_dr"], writes=[vk])

            def compute_head(h):
                hb = h % 2
                kk, vk, qk = "kT%d" % hb, "vaug%d" % hb, "qh%d" % hb
                for qb in range(4):
                    po, pok = psr(6, 8, octr)
                    NK = 66
                    sb_list = []

                    def s_mm(kt):
                        ps_, psk = psr(0, 6, sctr)
                        P.mm(ps_[:, :], kT[hb][:, kt * 128:(kt + 1) * 128], qh[hb][:, qb * 512:(qb + 1) * 512], True, True, [kk, qk], [psk])
                        sb_list.append((ps_, psk))
                    s_mm(0)
                    s_mm(1)
                    for kt in range(NK):
                        if kt + 2 < NK:
                            s_mm(kt + 2)
                        ps_, psk = sb_list[kt]
                        pi = kt % 4
                        P.act(pT[pi][:], ps_[:, :], AF.Exp, [psk], ["pT%d" % pi])
                        P.mm(po[:, :], vaug[hb][:, kt, :], pT[pi][:], kt == 0, kt == NK - 1, [vk, "pT%d" % pi], [pok])
                    o = (h * 4 + qb) % 2
                    P.copy(OS[o][:], po[:, :], [pok], ["OS%d" % o], eng="scalar")
                    src = slice(64, 128) if hb == 0 else slice(0, 64)
                    dst = slice(0, 64) if hb == 0 else slice(64, 128)
                    P.recip(Rr[src, :], OS[o][src, :], ["OS%d" % o], ["Rr"])
                    pn, pnk = psr(0, 6, sctr)
                    P.mm(pn[:, :], shiftm[src, :], Rr[src, :], True, True, ["shiftm", "Rr"], [pnk])
                    c = h // 2
                    P.tt(ao[hb][dst, :], OS[o][dst, :], pn[dst, :], ALU.mult, ["OS%d" % o, pnk], ["ao%d" % hb])
                    P.dma("sync", a_dr[c, dst, qb * 512:(qb + 1) * 512], ao[hb][dst, :], reads=["ao%d" % hb], writes=["a_dr"])
            for h in range(H):
                load_head(h)
                compute_head(h)
            if debug:
                P.dma("sync", dbg["a"], a_dr, reads=["a_dr"])

        P.barrier()
        FSB = 1024

        def ffn(l, xT, sF):
            hT = sb("f_hT%d" % l, [128, 8, FSB], BF16, sF)
            AT = sb("f_AT%d" % l, [128, NFF, FSB], BF16, sF)
            w1b = [sb("f_w1b%d_%d" % (l, i), [128, 8, 128], BF16, sF) for i in range(2)]
            w3b = [sb("f_w3b%d_%d" % (l, i), [128, 8, 128], BF16, sF) for i in range(2)]
            w2b = [sb("f_w2b%d_%d" % (l, i), [128, NFF, 128], BF16, sF) for i in range(2)]
            sg = [sb("f_sg%d_%d" % (l, i), [128, 512], F32, sF) for i in range(2)]
            sq = sb("f_sq%d" % l, [128, 2, 512], BF16, sF)
            rs = sb("f_rs%d" % l, [128, 512], F32, sF)
            tmpf = [sb("f_tmpf%d_%d" % (l, i), [128, 512], F32, sF) for i in range(2)]
            n = 0
            ng = 0
            for sbk in range(NT // FSB):
                for hf in range(FSB // 512):
                    t0 = sbk * FSB + hf * 512
                    rstd, rk = pnorm([xT[:, fc, t0:t0 + 512] for fc in range(8)], ["xT"] * 8, D, 512, (sq, rs))
                    for fc in range(8):
                        o = fc % 2
                        P.stt(tmpf[o][:], xT[:, fc, t0:t0 + 512], A2[:, l, fc, 0:1], rstd, ALU.mult, ALU.mult, ["xT", rk, "A2"], ["f_tmpf%d" % o])
                        P.act(hT[:, fc, hf * 512:(hf + 1) * 512], tmpf[o][:], AF.Identity, ["f_tmpf%d" % o, "mod"], ["f_hT"], bias=mv(l, 3, fc, 0))
                for jf in range(NFF):
                    i = n % 2
                    n += 1
                    load_cast(w1b[i][:].rearrange("p a b -> p (a b)"), w1_d[l, jf], 1024, "f_w1b%d" % i)
                    load_cast(w3b[i][:].rearrange("p a b -> p (a b)"), w3_d[l, jf], 1024, "f_w3b%d" % i)
                    for hf in range(FSB // 512):
                        cs_ = slice(hf * 512, (hf + 1) * 512)
                        pg, pgk = psum()
                        for kc in range(8):
                            P.mm(pg[:, :], w1b[i][:, kc, :], hT[:, kc, cs_], kc == 0, kc == 7, ["f_w1b%d" % i, "f_hT"], [pgk])
                        pu, puk = psum()
                        for kc in range(8):
                            P.mm(pu[:, :], w3b[i][:, kc, :], hT[:, kc, cs_], kc == 0, kc == 7, ["f_w3b%d" % i, "f_hT"], [puk])
                        g_ = ng % 2
                        ng += 1
                        P.act(sg[g_][:], pg[:, :], AF.Silu, [pgk], ["f_sg%d" % g_])
                        P.tt(AT[:, jf, cs_], sg[g_][:], pu[:, :], ALU.mult, ["f_sg%d" % g_, puk], ["f_AT%d" % jf])
                atk = ["f_AT%d" % jf for jf in range(NFF)]
                for m in range(8):
                    i = n % 2
                    n += 1
                    load_cast(w2b[i][:].rearrange("p a b -> p (a b)"), w2_d[l, m], NFF * 128, "f_w2b%d" % i)
                    for hf in range(FSB // 512):
                        t0 = sbk * FSB + hf * 512
                        po, pok = psum()
                        for jf in range(NFF):
                            P.mm(po[:, :], w2b[i][:, jf, :], AT[:, jf, hf * 512:(hf + 1) * 512], jf == 0, jf == NFF - 1, ["f_w2b%d" % i, "f_AT%d" % jf], [pok])
                        P.stt(xT[:, m, t0:t0 + 512], po[:, :], mv(l, 5, m, 0), xT[:, m, t0:t0 + 512], ALU.mult, ALU.add, [pok, "mod", "xT"], ["xT"])

        def proj_res(l, wd, srcT, srck, xT, sP):
            wb = [sb("p_wb%d_%d" % (l, i), [128, 8, 128], BF16, sP) for i in range(2)]
            for m in range(8):
                i = m % 2
                load_cast(wb[i][:].rearrange("p a b -> p (a b)"), wd[m], 1024, "p_wb%d" % i)
                for blk in range(4):
                    po, pok = psum()
                    for kc in range(8):
                        P.mm(po[:, :], wb[i][:, kc, :], srcT[:, kc, blk * 512:(blk + 1) * 512], kc == 0, kc == 7, ["p_wb%d" % i, srck], [pok])
                    P.stt(xT[:, m, blk * 512:(blk + 1) * 512], po[:, :], mv(l, 2, m, 0), xT[:, m, blk * 512:(blk + 1) * 512], ALU.mult, ALU.add, [pok, "mod", "xT"], ["xT"])

        def load_xT(xT, rows_of_tile, sX):
            stage = sb("x_stage%d" % rows_of_tile[1], [128, D], F32, sX)
            for tt_ in range(NT // 128):
                P.dma("sync", stage[:], rows_of_tile[0](tt_), writes=["x_stage"])
                for g in range(2):
                    ps, pk = psum()
                    for f4 in range(4):
                        fc = g * 4 + f4
                        P.op("tensor", lambda e, fc=fc, f4=f4, ps=ps: e.transpose(ps[:, f4 * 128:(f4 + 1) * 128], stage[:, fc * 128:(fc + 1) * 128], ident[:]),
                             ["x_stage", "ident"], [pk])
                    P.copy(xT[:, g * 4:(g + 1) * 4, tt_ * 128:(tt_ + 1) * 128], ps[:].rearrange("p (f t) -> p f t", t=128), [pk], ["xT"], eng="scalar")

        with ExitStack() as sC:
            xT = sb("xT", [128, 8, NT], F32, sC)
            with ExitStack() as sC1:
                load_xT(xT, (lambda tt_: x_d[tt_ * 128:(tt_ + 1) * 128, :], 0), sC1)
                mixT = sb("mixT", [128, 8, NT], BF16, sC1)
                for c in range(4):
                    P.dma("sync", mixT[:, c, :], a_dr[c], reads=["a_dr"], writes=["mixT"])
                    P.dma("sync", mixT[:, 4 + c, :], s_dr[c, :, 1:NT + 1], reads=["s_dr"], writes=["mixT"])
                proj_res(0, wo_d, mixT, "mixT", xT, sC1)
            P.barrier()
            with ExitStack() as sC2:
                ffn(0, xT, sC2)
                sq = sb("c_sq", [128, 2, 512], BF16, sC2)
                rs = sb("c_rs", [128, 512], F32, sC2)
                tmpf = sb("c_tmpf", [128, 512], F32, sC2)
                ho = [sb("c_ho%d" % i, [128, 512], BF16, sC2) for i in range(2)]
                for blk in range(4):
                    t0 = blk * 512
                    rstd, rk = pnorm([xT[:, fc, t0:t0 + 512] for fc in range(8)], ["xT"] * 8, D, 512, (sq, rs))
                    for fc in range(8):
                        o = fc % 2
                        P.stt(tmpf[:], xT[:, fc, t0:t0 + 512], A1[:, 1, fc, 0:1], rstd, ALU.mult, ALU.mult, ["xT", rk, "A1"], ["c_tmpf"])
                        P.act(ho[o][:], tmpf[:], AF.Identity, ["c_tmpf", "mod"], ["c_ho%d" % o], bias=mv(1, 0, fc, 0))
                        P.dma("sync", h1_in.ap()[fc * 128:(fc + 1) * 128, t0:t0 + 512], ho[o][:], reads=["c_ho%d" % o], writes=["h1_in"])
                for fc in range(8):
                    P.dma("sync", x1_dr[fc], xT[:, fc, :], reads=["xT"], writes=["x1_dr"])
            if debug:
                P.dma("sync", dbg["x1"], x1_dr, reads=["x1_dr"])
                P.dma("sync", dbg["h1"], h1_in.ap(), reads=["h1_in"])
        P.barrier()
        P.coll(h1_in.ap(), h1_all.ap(), ["h1_in"], ["h1_all"])
        h1a = h1_all.ap()

        with ExitStack() as sD:
            XT = sb("XT", [128, 2, S], BF16, sD)
            cst = [sb("d_cst%d" % i, [128, NT], BF16, sD) for i in range(4)]
            dzer = sb("dzer", [128, NT], BF16, sD)
            P.memset(dzer[:], 0.0, ["dzer"])
            fcb = sb("fcb", [128, 2, 4, 128], BF16, sD)
            wab = sb("wab", [128, 2, 256], BF16, sD)
            mpb = sb("mpb", [64, 2, 128, 64], BF16, sD)
            Zsb = sb("Zsb", [128, 64, 128], BF16, sD)
            Ysb = sb("Ysb", [64, 64, 256], BF16, sD)
            Fsb = sb("Fsb", [64, 128, 64], BF16, sD)
            load_cast(fcb[:].rearrange("p a b c -> p (a b c)"), fc_d.rearrange("p a b c -> p (a b c)"), 1024, "fcb")
            load_cast(wab[:].rearrange("p a b -> p (a b)"), wa_d.rearrange("p a b -> p (a b)"), 512, "wab")
            mpf = mpb[:].rearrange("p a b c -> p (a b c)")
            mpd = mp_d.rearrange("p a b c -> p (a b c)")
            for i4 in range(4):
                load_cast(mpf[:, i4 * 4096:(i4 + 1) * 4096], mpd[:, i4 * 4096:(i4 + 1) * 4096], 4096, "mpb")
            nsel = 0
            for r in range(4):
                for cc in range(2):
                    dstX = XT[:, cc, r * NT:(r + 1) * NT]
                    for k in range(8):
                        bb_, gg = k // 4, k % 4
                        i = nsel % 4
                        nsel += 1
                        R = 4 * bb_ + r
                        row0 = R * D + gg * 256 + cc * 128
                        P.dma("sync", cst[i][:], h1a[row0:row0 + 128, :], reads=["h1_all"], writes=["d_cst%d" % i])
                        sel = vecs[:, V_SEL + k:V_SEL + k + 1]
                        P.stt(dstX, cst[i][:], sel, (dzer[:] if k == 0 else dstX), ALU.mult, ALU.add, ["d_cst%d" % i, "vecs", "XT%d%d" % (r, cc), "dzer"], ["XT%d%d" % (r, cc)])
            xkeys = ["XT%d%d" % (r, cc) for r in range(4) for cc in range(2)]
            fview = f_in.ap().rearrange("(q p) m -> q p m", p=128)
            ne = 0
            for mc in range(4):
                for b4 in range(16):
                    ps, pk = psum()
                    for bi in range(4):
                        bp = b4 * 4 + bi
                        for cc in range(2):
                            P.mm(ps[:, bi * 128:(bi + 1) * 128], XT[:, cc, bp::64], fcb[:, cc, mc, :], cc == 0, cc == 1, xkeys + ["fcb"], [pk])
                    ne += 1
                    P.copy(Zsb[:, b4 * 4:(b4 + 1) * 4, :], ps[:].rearrange("p (b m) -> p b m", m=128), [pk], ["Zsb"], eng="scalar" if ne % 2 else "vector")
                for m2 in range(32):
                    ps, pk = psum()
                    for mi in range(2):
                        m = m2 * 2 + mi
                        P.mm(ps[0:64, mi * 256:(mi + 1) * 256], Zsb[:, :, m], wab[:, 0, :], True, False, ["Zsb", "wab"], [pk])
                        P.mm(ps[0:64, mi * 256:(mi + 1) * 256], Zsb[:, :, 64 + m], wab[:, 1, :], False, True, ["Zsb", "wab"], [pk])
                    ne += 1
                    P.copy(Ysb[:, m2 * 2:(m2 + 1) * 2, :], ps[0:64, :].rearrange("p (m c) -> p m c", c=256), [pk], ["Ysb"], eng="scalar" if ne % 2 else "vector")
                for p8 in range(16):
                    ps, pk = psum()
                    for pi in range(8):
                        p = p8 * 8 + pi
                        P.mm(ps[0:64, pi * 64:(pi + 1) * 64], mpb[:, 0, p, :], Ysb[:, :, p], True, False, ["mpb", "Ysb"], [pk])
                        P.mm(ps[0:64, pi * 64:(pi + 1) * 64], mpb[:, 1, p, :], Ysb[:, :, 128 + p], False, True, ["mpb", "Ysb"], [pk])
                    ne += 1
                    P.copy(Fsb[:, p8 * 8:(p8 + 1) * 8, :], ps[0:64, :].rearrange("p (a m) -> p a m", m=64), [pk], ["Fsb"], eng="scalar" if ne % 2 else "vector")
                P.dma("sync", fview[:, :, mc * 64:(mc + 1) * 64], Fsb[:], reads=["Fsb"], writes=["f_in"])
            if debug:
                P.dma("sync", dbg["f"], f_in.ap(), reads=["f_in"])
        P.barrier()
        P.coll(f_in.ap(), f_all.ap(), ["f_in"], ["f_all"])
        fa = f_all.ap()

        with ExitStack() as sE:
            xT = sb("xT2", [128, 8, NT], F32, sE)
            for fc in range(8):
                P.dma("sync", xT[:, fc, :], x1_dr[fc], reads=["x1_dr"], writes=["xT"])
            with ExitStack() as sE1:
                FT = sb("FT", [128, 8, NT], BF16, sE1)
                est = [sb("e_st%d" % i, [128, 16, 256], BF16, sE1) for i in range(2)]
                facc = sb("e_acc", [128, 16, 256], F32, sE1)
                ezer = sb("ezer", [128, 16, 256], BF16, sE1)
                P.memset(ezer[:], 0.0, ["ezer"])
                nsel = 0
                for g in range(4):
                    for k in range(8):
                        bb_, jj = k // 4, k % 4
                        i = nsel % 2
                        nsel += 1
                        row0 = (4 * bb_ + g) * S + jj * NT
                        P.dma("sync", est[i][:], fa[row0:row0 + NT, :].rearrange("(t p) m -> p t m", p=128), reads=["f_all"], writes=["e_st%d" % i])
                        sel = vecs[:, V_SEL + k:V_SEL + k + 1]
                        P.stt(facc[:], est[i][:], sel, (ezer[:] if k == 0 else facc[:]), ALU.mult, ALU.add, ["e_st%d" % i, "vecs", "e_acc", "ezer"], ["e_acc"])
                    for tt_ in range(16):
                        ps, pk = psum()
                        for c2 in range(2):
                            P.op("tensor", lambda e, c2=c2, ps=ps, tt_=tt_: e.transpose(ps[:, c2 * 128:(c2 + 1) * 128], facc[:, tt_, c2 * 128:(c2 + 1) * 128], ident[:]),
                                 ["e_acc", "ident"], [pk])
                        P.copy(FT[:, g * 2:(g + 1) * 2, tt_ * 128:(tt_ + 1) * 128], ps[:, 0:256].rearrange("p (f t) -> p f t", t=128), [pk], ["FT"], eng="scalar")
                proj_res(1, wf_d, FT, "FT", xT, sE1)
            P.barrier()
            with ExitStack() as sE2:
                ffn(1, xT, sE2)
                ost = [sb("o_st%d" % i, [128, D], F32, sE2) for i in range(2)]
                for tt_ in range(NT // 128):
                    o = tt_ % 2
                    for g in range(2):
                        ps, pk = psum()
                        for f4 in range(4):
                            fc = g * 4 + f4
                            P.op("tensor", lambda e, fc=fc, f4=f4, ps=ps, tt_=tt_: e.transpose(ps[:, f4 * 128:(f4 + 1) * 128], xT[:, fc, tt_ * 128:(tt_ + 1) * 128], ident[:]),
                                 ["xT", "ident"], [pk])
                        P.copy(ost[o][:, g * 512:(g + 1) * 512], ps[:, :], [pk], ["o_st%d" % o], eng="scalar" if g else "vector")
                    P.dma("sync", out_d[tt_ * 128:(tt_ + 1) * 128, :], ost[o][:], reads=["o_st%d" % o], writes=["out"])

        P.wait("sync", [t for t in P.dlast if t is not None])
        P.emit()
    return nc


def _rope_tables(tok0):
    t = np.arange(tok0, tok0 + NT)
    row = (t // 64).astype(np.float32)
    col = (t % 64).astype(np.float32)
    half = 16
    inv = (1.0 / (10000.0 ** (np.arange(0, half, 2, dtype=np.float32) / half))).astype(np.float32)
    ar = row[:, None] * inv
    ac = col[:, None] * inv
    C = np.ones((96, NT), np.float32)
    Sn = np.zeros((96, NT), np.float32)
    for r in range(32):
        ang = ar if r < 16 else ac
        C[64 + r] = np.cos(ang[:, r % 8])
        Sn[64 + r] = np.sin(ang[:, r % 8])
    return C, Sn


def _consts():
    ident = np.eye(128, dtype=np.float32)
    pm = np.zeros((96, 96), np.float32)
    for r in range(32):
        d = 64 + r
        if (r % 16) < 8:
            pm[d, d + 8] = -1.0
        else:
            pm[d, d - 8] = 1.0
    pmT = np.ascontiguousarray(pm.T)
    shift = np.zeros((128, 128), np.float32)
    for k in range(128):
        shift[k, (k + 64) % 128] = 1.0
    c = np.arange(256)
    ang = 2 * np.pi * np.outer(c, c) / 256.0
    fcm = np.concatenate([np.cos(ang), -np.sin(ang)], 1).astype(np.float32)
    fcm = np.ascontiguousarray(fcm.reshape(2, 128, 2, 4, 64).transpose(1, 0, 3, 2, 4).reshape(128, 2, 4, 128))
    a = np.arange(128)
    ang = 2 * np.pi * np.outer(a, a) / 128.0
    Cm, Sm = np.cos(ang), np.sin(ang)
    wam = np.stack([np.concatenate([Cm, -Sm], 1), np.concatenate([Sm, Cm], 1)], 1).astype(np.float32)
    bq = np.arange(64)
    p = np.arange(128)
    ang = 2 * np.pi * (bq[:, None, None] * bq[None, None, :] / 64.0 + bq[:, None, None] * p[None, :, None] / 8192.0)
    sc = 1.0 / np.sqrt(8192.0 * 256.0)
    Mr = np.cos(ang) * sc
    Mi = -np.sin(ang) * sc
    mpm = np.stack([Mr, -Mi], 1).astype(np.float32)
    return dict(ident=ident, pmT=pmT, shiftm=shift, fcm=fcm, wam=wam, mpm=mpm)


def _prep_inputs(x, c, ctx, c_ctx, ada_w, ada_b, norm1_g, norm2_g, w_in, q_norm_g, kv_norm_g, w_uq, w_ukv,
                 q_gain, k_gain, conv_w, w_o, w_fourier, ffn_w1, ffn_w3, ffn_w2):
    f = lambda a: np.ascontiguousarray(np.asarray(a, dtype=np.float32))
    x, c, ctx, c_ctx = f(x), f(c), f(ctx), f(c_ctx)
    shared = _consts()
    shared["ada_w"] = f(np.asarray(ada_w).reshape(2, 8, 128, 12, 512).transpose(0, 3, 2, 1, 4).reshape(2, 12, 128, 8 * 512))
    shared["w_in"] = f(np.asarray(w_in)[0].reshape(8, 128, 2208).transpose(1, 0, 2))
    shared["w_uq"] = f(np.asarray(w_uq)[0].reshape(3, 128, 768).transpose(1, 0, 2))
    shared["w_ukv"] = f(np.asarray(w_ukv)[0].reshape(2, 128, 1024).transpose(1, 0, 2))
    colblk = lambda w, kc, nm: f(np.asarray(w).reshape(kc, 128, nm, 128).transpose(2, 1, 0, 3).reshape(nm, 128, kc * 128))
    shared["w_o"] = colblk(np.asarray(w_o)[0], 8, 8)
    shared["w_f"] = colblk(np.asarray(w_fourier)[0], 8, 8)
    shared["ffn_w1"] = np.stack([colblk(np.asarray(ffn_w1)[l], 8, NFF) for l in range(2)])
    shared["ffn_w3"] = np.stack([colblk(np.asarray(ffn_w3)[l], 8, NFF) for l in range(2)])
    shared["ffn_w2"] = np.stack([colblk(np.asarray(ffn_w2)[l], NFF, 8) for l in range(2)])
    in_maps = []
    for core in range(NCORES):
        b, j = core // 4, core % 4
        t0 = j * NT
        m = dict(shared)
        m["x"] = f(x[b, t0:t0 + NT])
        xh = np.zeros((2, D), np.float32)
        hm = np.zeros((2,), np.float32)
        if j > 0:
            xh[0] = x[b, t0 - 1]
            hm[0] = 1.0
        if j < 3:
            xh[1] = x[b, t0 + NT]
            hm[1] = 1.0
        m["xh"] = xh
        m["ctx"] = f(ctx[b])
        vecs = np.zeros((128, NVEC), np.float32)
        cs = np.stack([c[b], c_ctx], 0)
        vecs[:, V_CS:V_CS + 16] = cs.reshape(2, 8, 128).transpose(2, 1, 0).reshape(128, 16)
        for l in range(2):
            vecs[:, V_ADAB + 48 * l:V_ADAB + 48 * (l + 1)] = np.asarray(ada_b)[l].reshape(48, 128).T
            vecs[:, V_N1 + 8 * l:V_N1 + 8 * (l + 1)] = np.asarray(norm1_g)[l].reshape(8, 128).T
            vecs[:, V_N2 + 8 * l:V_N2 + 8 * (l + 1)] = np.asarray(norm2_g)[l].reshape(8, 128).T
        vecs[:, V_QNG:V_QNG + 3] = np.asarray(q_norm_g)[0].reshape(3, 128).T
        vecs[:, V_KVNG:V_KVNG + 2] = np.asarray(kv_norm_g)[0].reshape(2, 128).T
        vecs[0:96, V_QG] = np.asarray(q_gain)[0]
        vecs[0:96, V_KG] = np.asarray(k_gain)[0]
        vecs[:, V_CONV:V_CONV + 12] = np.asarray(conv_w)[0].reshape(3, 4, 128).transpose(2, 0, 1).reshape(128, 12)
        vecs[:, V_HM:V_HM + 2] = hm[None, :]
        vecs[:, V_MB + b] = 1.0
        vecs[:, V_SEL + core] = 1.0
        m["vecs"] = vecs
        C, Sn = _rope_tables(t0)
        m["ropec"], m["ropes"] = C, Sn
        in_maps.append(m)
    return in_maps


def kernel(**inputs):
    in_maps = _prep_inputs(**inputs)
    nc = build()
    res = run_bass_kernel_spmd(nc, in_maps, core_ids=list(range(NCORES)))
    out = np.zeros((NB, S, D), np.float32)
    for core in range(NCORES):
        b, j = core // 4, core % 4
        out[b, j * NT:(j + 1) * NT] = res.results[core]["out"]
    return out
```
